# Optimizing a Trainium2 kernel written in Bass

```python
import math
import jax
import jax.numpy as jnp
from jax import lax
import numpy as np

D_MODEL = 1024
BATCH = 4
SEQ = 4096
DEPTH = 1

GLA_HEADS = 4
GLA_DK = 128
GLA_DV = 256
GLA_GATE_RANK = 16
GLA_TAU = 16.0
GLA_CHUNK = 64
NSA_HEADS = 8
NSA_GROUPS = 2
NSA_REP = NSA_HEADS // NSA_GROUPS
NSA_DH = 64
NSA_CMP_LEN = 32
NSA_CMP_STRIDE = 16
NSA_SLC_LEN = 64
NSA_N_SEL = 16
NSA_WINDOW = 512
NSA_Q_BLOCK = 128
D_FF = 2816
RMS_EPS = 1e-6
NEG = -1e30

GLA_QK_W = GLA_HEADS * GLA_DK
GLA_V_W = GLA_HEADS * GLA_DV
NSA_Q_W = NSA_HEADS * NSA_DH
NSA_KV_W = NSA_GROUPS * NSA_DH
NSA_GATE_W = NSA_HEADS * 3
IN_SPLITS = (GLA_QK_W, GLA_QK_W, GLA_V_W, GLA_V_W, GLA_GATE_RANK,
             NSA_Q_W, NSA_KV_W, NSA_KV_W, NSA_KV_W, NSA_KV_W, NSA_KV_W, NSA_KV_W,
             NSA_GATE_W, D_MODEL, D_MODEL)
IN_WIDTH = 4 * GLA_QK_W // 2 + 2 * GLA_V_W + GLA_GATE_RANK + NSA_Q_W + 6 * NSA_KV_W + NSA_GATE_W + 2 * D_MODEL

kernel_name = 'gla_nsa_hybrid_block'


def rms_norm(x, g):
    xf = x.astype(jnp.float32)
    y = xf * lax.rsqrt(jnp.mean(xf * xf, axis=-1, keepdims=True) + RMS_EPS)
    return (y * g.astype(jnp.float32)).astype(x.dtype)


def masked_softmax(s, mask):
    s = jnp.where(mask, s, NEG)
    p = jax.nn.softmax(s, axis=-1)
    return jnp.where(mask, p, 0.0)


def gla_mixer(q, k, v, r, a_lr, w_alpha2, b_alpha, norm_g):
    B, S = q.shape[0], q.shape[1]
    C = GLA_CHUNK
    nc = S // C

    def heads(t, d):
        return t.reshape(B, nc, C, GLA_HEADS, d).transpose(0, 3, 1, 2, 4).astype(jnp.float32)

    qh = heads(q, GLA_DK) * (GLA_DK ** -0.5)
    kh = heads(k, GLA_DK)
    vh = heads(v, GLA_DV)
    log_a = jax.nn.log_sigmoid((a_lr @ w_alpha2 + b_alpha).astype(jnp.float32)) / GLA_TAU
    b = jnp.cumsum(heads(log_a, GLA_DK), axis=3)
    b_last = b[:, :, :, -1:, :]
    qe = qh * jnp.exp(b)
    ke = kh * jnp.exp(-b)
    kd = kh * jnp.exp(b_last - b)
    causal = jnp.tril(jnp.ones((C, C), dtype=bool))
    attn = jnp.where(causal, jnp.einsum('bhncd,bhnsd->bhncs', qe, ke), 0.0)
    o = jnp.einsum('bhncs,bhnse->bhnce', attn, vh)
    upd = jnp.einsum('bhncd,bhnce->nbhde', kd, vh)
    decay = jnp.exp(b_last[:, :, :, 0, :]).transpose(2, 0, 1, 3)

    def step(state, inp):
        dec, u = inp
        return dec[..., None] * state + u, state

    init = jnp.zeros((B, GLA_HEADS, GLA_DK, GLA_DV), jnp.float32)
    _, s_prev = lax.scan(step, init, (decay, upd))
    o = o + jnp.einsum('bhncd,nbhde->bhnce', qe, s_prev)
    o = o * lax.rsqrt(jnp.mean(o * o, axis=-1, keepdims=True) + RMS_EPS) * norm_g.astype(jnp.float32)
    o = o.reshape(B, GLA_HEADS, S, GLA_DV).transpose(0, 2, 1, 3).reshape(B, S, GLA_V_W)
    return (o * jax.nn.silu(r.astype(jnp.float32))).astype(r.dtype)


def nsa_mixer(q, k_c, v_c, k_s, v_s, k_w, v_w, gate_logits, pe_k, w1_k, w2_k, pe_v, w1_v, w2_v):
    B, S = q.shape[0], q.shape[1]
    G, R, DH = NSA_GROUPS, NSA_REP, NSA_DH
    L, ST, SL, QB, W = NSA_CMP_LEN, NSA_CMP_STRIDE, NSA_SLC_LEN, NSA_Q_BLOCK, NSA_WINDOW
    qh = q.reshape(B, S, G, R, DH).transpose(0, 2, 3, 1, 4).astype(jnp.float32) * (DH ** -0.5)

    def kv_heads(t):
        return t.reshape(B, S, G, DH).transpose(0, 2, 1, 3).astype(jnp.float32)

    gates = jax.nn.sigmoid(gate_logits.astype(jnp.float32)).reshape(B, S, G, R, 3).transpose(0, 2, 3, 1, 4)
    h_idx = jnp.arange(NSA_HEADS, dtype=jnp.float32)
    slopes = jnp.exp2(-8.0 * (h_idx + 1.0) / NSA_HEADS).reshape(1, G, R, 1, 1)

    n_cmp = (S - L) // ST + 1
    starts_c = ST * jnp.arange(n_cmp)
    idx_c = starts_c[:, None] + jnp.arange(L)[None, :]

    def compress(t, pe, w1, w2):
        blocks = t[:, :, idx_c] + pe
        flat = blocks.reshape(B, G, n_cmp, L * DH)
        return jax.nn.silu(flat @ w1) @ w2

    kc = compress(kv_heads(k_c), pe_k, w1_k, w2_k).astype(jnp.float32)
    vc = compress(kv_heads(v_c), pe_v, w1_v, w2_v).astype(jnp.float32)
    end_c = starts_c + L - 1

    n_slc = S // SL
    n_sel = min(NSA_N_SEL, n_slc)
    starts_s = SL * jnp.arange(n_slc)
    ov = jnp.clip(jnp.minimum(starts_c[:, None] + L, starts_s[None, :] + SL)
                  - jnp.maximum(starts_c[:, None], starts_s[None, :]), 0, None).astype(jnp.float32) / L
    ks_blocks = kv_heads(k_s).reshape(B, G, n_slc, SL, DH)
    vs_blocks = kv_heads(v_s).reshape(B, G, n_slc, SL, DH)
    gather = jax.vmap(jax.vmap(lambda blk, ix: blk[ix]))

    kw_pad = jnp.pad(kv_heads(k_w), ((0, 0), (0, 0), (W, 0), (0, 0)))
    vw_pad = jnp.pad(kv_heads(v_w), ((0, 0), (0, 0), (W, 0), (0, 0)))

    nq = S // QB
    q_blocks = qh.reshape(B, G, R, nq, QB, DH).transpose(3, 0, 1, 2, 4, 5)
    g_blocks = gates.reshape(B, G, R, nq, QB, 3).transpose(3, 0, 1, 2, 4, 5)
    jb = jnp.arange(n_slc)
    r_w = jnp.arange(W + QB)

    def attend_block(args):
        i, qb, gb = args
        q0 = i * QB
        t = q0 + jnp.arange(QB)
        tf = t.astype(jnp.float32)
        dist_c = tf[:, None] - end_c[None, :].astype(jnp.float32)
        s_c = jnp.einsum('bgrqd,bgnd->bgrqn', qb, kc) - slopes * dist_c
        p_c = masked_softmax(s_c, end_c[None, :] <= t[:, None])
        o_c = jnp.einsum('bgrqn,bgnd->bgrqd', p_c, vc)
        imp = jnp.einsum('bgrqn,nj->bgqj', p_c, ov)
        cur = t // SL
        forced = (jb[None, :] == 0) | (jb[None, :] == cur[:, None]) | (jb[None, :] == cur[:, None] - 1)
        score = jnp.where(jb[None, :] > cur[:, None], NEG, jnp.where(forced, -NEG, imp))
        _, sel = lax.top_k(score, n_sel)
        ks = gather(ks_blocks, sel).reshape(B, G, QB, n_sel * SL, DH)
        vs = gather(vs_blocks, sel).reshape(B, G, QB, n_sel * SL, DH)
        pos_s = (sel[..., None] * SL + jnp.arange(SL)).reshape(B, G, QB, n_sel * SL)
        dist_s = (tf[None, None, :, None] - pos_s.astype(jnp.float32))[:, :, None]
        s_s = jnp.einsum('bgrqd,bgqkd->bgrqk', qb, ks) - slopes * dist_s
        p_s = masked_softmax(s_s, (pos_s <= t[None, None, :, None])[:, :, None])
        o_s = jnp.einsum('bgrqk,bgqkd->bgrqd', p_s, vs)
        kw = lax.dynamic_slice_in_dim(kw_pad, q0, W + QB, axis=2)
        vw = lax.dynamic_slice_in_dim(vw_pad, q0, W + QB, axis=2)
        pos_w = q0 - W + r_w
        dist_w = t[:, None] - pos_w[None, :]
        mask_w = (pos_w[None, :] >= 0) & (dist_w >= 0) & (dist_w < W)
        s_w = jnp.einsum('bgrqd,bgkd->bgrqk', qb, kw) - slopes * dist_w.astype(jnp.float32)
        p_w = masked_softmax(s_w, mask_w)
        o_w = jnp.einsum('bgrqk,bgkd->bgrqd', p_w, vw)
        return gb[..., 0:1] * o_c + gb[..., 1:2] * o_s + gb[..., 2:3] * o_w

    o = lax.map(attend_block, (jnp.arange(nq), q_blocks, g_blocks))
    return o.transpose(1, 0, 4, 2, 3, 5).reshape(B, S, NSA_Q_W).astype(q.dtype)


def setup_inputs(seed: int = 0) -> dict:
    key = jax.random.key(seed)
    ks = jax.random.split(key, 24)
    f32 = jnp.float32

    def normal(k, shape, fan_in):
        return jax.random.normal(k, shape, f32) * (fan_in ** -0.5)

    def gain(k, shape):
        return 1.0 + 0.1 * jax.random.normal(k, shape, f32)

    Ld = DEPTH
    return {
        'x': jax.random.normal(ks[0], (BATCH, SEQ, D_MODEL), f32),
        'norm_mix_pre': gain(ks[1], (Ld, D_MODEL)),
        'norm_mix_post': gain(ks[2], (Ld, D_MODEL)),
        'norm_ffn_pre': gain(ks[3], (Ld, D_MODEL)),
        'norm_ffn_post': gain(ks[4], (Ld, D_MODEL)),
        'w_in': normal(ks[5], (Ld, D_MODEL, IN_WIDTH), D_MODEL),
        'gla_w_alpha2': normal(ks[6], (Ld, GLA_GATE_RANK, GLA_QK_W), GLA_GATE_RANK),
        'gla_b_alpha': 0.1 * jax.random.normal(ks[7], (Ld, GLA_QK_W), f32),
        'gla_norm_g': gain(ks[8], (Ld, GLA_DV)),
        'nsa_cmp_pe_k': 0.1 * jax.random.normal(ks[9], (Ld, NSA_CMP_LEN, NSA_DH), f32),
        'nsa_cmp_w1_k': normal(ks[10], (Ld, NSA_CMP_LEN * NSA_DH, NSA_DH), NSA_CMP_LEN * NSA_DH),
        'nsa_cmp_w2_k': normal(ks[11], (Ld, NSA_DH, NSA_DH), NSA_DH),
        'nsa_cmp_pe_v': 0.1 * jax.random.normal(ks[12], (Ld, NSA_CMP_LEN, NSA_DH), f32),
        'nsa_cmp_w1_v': normal(ks[13], (Ld, NSA_CMP_LEN * NSA_DH, NSA_DH), NSA_CMP_LEN * NSA_DH),
        'nsa_cmp_w2_v': normal(ks[14], (Ld, NSA_DH, NSA_DH), NSA_DH),
        'w_proj_gla': normal(ks[15], (Ld, GLA_V_W, D_MODEL), GLA_V_W),
        'w_proj_nsa': normal(ks[16], (Ld, NSA_Q_W, D_MODEL), NSA_Q_W),
        'w_out': normal(ks[17], (Ld, D_MODEL, D_MODEL), D_MODEL),
        'w_ffn_gate': normal(ks[18], (Ld, D_MODEL, D_FF), D_MODEL),
        'w_ffn_up': normal(ks[19], (Ld, D_MODEL, D_FF), D_MODEL),
        'w_ffn_down': normal(ks[20], (Ld, D_FF, D_MODEL), D_FF),
    }


def reference(x, norm_mix_pre, norm_mix_post, norm_ffn_pre, norm_ffn_post, w_in,
              gla_w_alpha2, gla_b_alpha, gla_norm_g,
              nsa_cmp_pe_k, nsa_cmp_w1_k, nsa_cmp_w2_k, nsa_cmp_pe_v, nsa_cmp_w1_v, nsa_cmp_w2_v,
              w_proj_gla, w_proj_nsa, w_out, w_ffn_gate, w_ffn_up, w_ffn_down):
    split_points = [int(c) for c in np.cumsum(IN_SPLITS)[:-1]]
    for l in range(DEPTH):
        h = rms_norm(x, norm_mix_pre[l])
        proj = h @ w_in[l]
        (g_q, g_k, g_v, g_r, g_a, n_q, n_kc, n_vc, n_ks, n_vs, n_kw, n_vw,
         n_gate, merge_gla, merge_nsa) = jnp.split(proj, split_points, axis=-1)
        o_gla = gla_mixer(g_q, g_k, g_v, g_r, g_a, gla_w_alpha2[l], gla_b_alpha[l], gla_norm_g[l])
        o_nsa = nsa_mixer(n_q, n_kc, n_vc, n_ks, n_vs, n_kw, n_vw, n_gate,
                          nsa_cmp_pe_k[l], nsa_cmp_w1_k[l], nsa_cmp_w2_k[l],
                          nsa_cmp_pe_v[l], nsa_cmp_w1_v[l], nsa_cmp_w2_v[l])
        mixed = (jax.nn.sigmoid(merge_gla) * (o_gla @ w_proj_gla[l])
                 + jax.nn.sigmoid(merge_nsa) * (o_nsa @ w_proj_nsa[l]))
        x = x + rms_norm(mixed @ w_out[l], norm_mix_post[l])
        h = rms_norm(x, norm_ffn_pre[l])
        f = (jax.nn.silu(h @ w_ffn_gate[l]) * (h @ w_ffn_up[l])) @ w_ffn_down[l]
        x = x + rms_norm(f, norm_ffn_post[l])
    return x
```

```python
import contextlib
import numpy as np
import concourse.bass as bass
import concourse.mybir as mybir
from concourse.bass_utils import run_bass_kernel_spmd

F32, BF16 = mybir.dt.float32, mybir.dt.bfloat16
AF = mybir.ActivationFunctionType
ALU = mybir.AluOpType
NEGB = -30000.0
ENG = ("pe", "act", "dve", "pool", "sp")
NT_CTX, NT_OWN = 32, 16
EPS = 1e-6
D_FF = 2816
NFC = D_FF // 128


class Prog:
    def __init__(self, nc, n_dma_sems=16):
        self.nc = nc
        self.ops = []
        self.lastw = {}
        self.readers = {}
        self.deps = []
        self.n_dma_sems = n_dma_sems
        self.fence_deps = set()
        self.fence_pending = set()
        self.last_eng = {}
        self.last_dma = {}
        self.dma_rr = {e: 0 for e in ENG}
        self.slot = []
        self.excl = set()
        self.alias = {}

    def op(self, eng, fn, R=(), W=(), dma=False):
        i = len(self.ops)
        d = set()
        if self.alias:
            R = [p for k in R for p in self.alias.get(k, (k,))]
            W = [p for k in W for p in self.alias.get(k, (k,))]
        for k in R:
            if k in self.lastw:
                d.add(self.lastw[k])
            if k in self.excl:
                for r in self.readers.get(k, ()):
                    if self.ops[r][0] != eng:
                        d.add(r)
        for k in W:
            if k in self.lastw:
                d.add(self.lastw[k])
            for r in self.readers.get(k, ()):
                d.add(r)
        if eng in self.fence_pending:
            d |= self.fence_deps
            self.fence_pending.discard(eng)
        d.discard(i)
        for k in R:
            self.readers.setdefault(k, []).append(i)
        for k in W:
            self.lastw[k] = i
            self.readers[k] = []
        self.ops.append((eng, fn, dma))
        self.deps.append(d)
        if dma:
            s = (eng, self.dma_rr[eng] % (5 if eng == "pool" else self.n_dma_sems))
            self.dma_rr[eng] += 1
            self.slot.append(s)
            if s in self.last_dma:
                d.add(self.last_dma[s])
            self.last_dma[s] = i
        else:
            self.slot.append(None)
            self.last_eng[eng] = i
        return i

    def dma(self, q, out, in_, R=(), W=()):
        return self.op(q, lambda e: e.dma_start(out=out, in_=in_), R, W, dma=True)

    def fence(self):
        self.fence_deps = set(self.last_eng.values()) | set(self.last_dma.values())
        self.fence_pending = set(ENG)

    def emit(self, final_wait_ops=()):
        nc = self.nc
        ops, deps = self.ops, self.deps
        n = len(ops)
        for i in range(n):
            best = {}
            for j in deps[i]:
                k = self.slot[j] if ops[j][2] else ops[j][0]
                if k not in best or j > best[k]:
                    best[k] = j
            deps[i] = set(best.values())
        need = [False] * n
        for i in range(n):
            ei, _, di = ops[i]
            for j in deps[i]:
                ej, _, dj = ops[j]
                if dj or di or not (ei == "pe" and ej == "pe"):
                    need[j] = True
        for j in final_wait_ops:
            need[j] = True
        cnt = {e: 0 for e in ENG}
        dcnt = {}
        tok = [None] * n
        for i in range(n):
            e, _, d = ops[i]
            if d:
                s = self.slot[i]
                dcnt[s] = dcnt.get(s, 0) + 16
                tok[i] = (("dma",) + s, dcnt[s])
            elif need[i]:
                cnt[e] += 1
                tok[i] = (("eng", e), cnt[e])
        semkeys = sorted({t[0] for t in tok if t is not None})
        with contextlib.ExitStack() as st:
            sems = {k: st.enter_context(nc.semaphore("s_" + "_".join(map(str, k)))) for k in semkeys}
            block = st.enter_context(nc.Block())
            per = {e: [] for e in ENG}
            for i in range(n):
                per[ops[i][0]].append(i)
            final = {}
            for j in final_wait_ops:
                final.setdefault(ops[j][0], []).append(j)

            def body(e, handle):
                waited = {}
                for i in per[e]:
                    _, fn, d = ops[i]
                    for j in sorted(deps[i]):
                        t = tok[j]
                        if t is None:
                            continue
                        if e == "pe" and ops[j][0] == "pe" and not d and not ops[j][2]:
                            continue
                        if waited.get(t[0], 0) >= t[1]:
                            continue
                        handle.wait_ge(sems[t[0]], t[1])
                        waited[t[0]] = t[1]
                    ins = fn(handle)
                    if tok[i] is not None:
                        ins.then_inc(sems[tok[i][0]], 16 if d else 1)
                for j in final.get(e, ()):
                    t = tok[j]
                    if waited.get(t[0], 0) < t[1]:
                        handle.wait_ge(sems[t[0]], t[1])
                        waited[t[0]] = t[1]

            reg = {"pe": block.tensor, "act": block.scalar, "dve": block.vector,
                   "pool": block.gpsimd, "sp": block.sync}
            for e in ENG:
                if per[e] or final.get(e):
                    reg[e](lambda h, e=e: body(e, h))
        return {e: len(per[e]) for e in ENG}, cnt


IN_SPECS = dict(
    xc=(4096, 1024), w_in=(1024, 6440), walpha=(17, 512), ngrep=(128, 1024),
    gpre_b=(128, 1024), gffn_b=(128, 1024), gpost_rep=(128, 1024), gpost2_rep=(128, 1024),
    w1k=(64, 2048), w2k=(64, 64), pek=(128, 32, 2), w1v=(64, 2048), w2v=(64, 64), pev=(128, 32, 2),
    wpg=(1024, 1024), wpn=(512, 1024), wo=(1024, 1024),
    wfg=(NFC, 128, 1024), wfu=(NFC, 128, 1024), wfd=(D_FF, 1024),
    kaug=(5, 4096), qaug=(2, 16, 5, 512), caug=(5, 256), shc=(16, 16, 256), tc=(16, 512),
    ov=(128, 2, 64), selA=(16, 128, 64), selB=(16, 128, 64), eall=(64, 4096),
    ekt=(59, 4096), trib=(128, 512), trilo=(128, 512), ident=(128, 128), tril=(128, 128), gmask=(128, 512),
)


def build(stop_after=4, dbg=False):
    nc = bass.Bass("TRN2", target_bir_lowering=False)
    I = {k: nc.dram_tensor(k, list(s), F32, kind="ExternalInput").ap() for k, s in IN_SPECS.items()}
    out = nc.dram_tensor("out", [2048, 1024], F32, kind="ExternalOutput").ap()
    skind = "ExternalOutput" if dbg else "Internal"
    hT_scr = nc.dram_tensor("hT_scr", [NT_CTX, 128, 1024], BF16, kind=skind).ap()
    E_scr = nc.dram_tensor("E_scr", [NT_CTX, 128, 512], F32, kind="Internal").ap()
    mixA_scr = nc.dram_tensor("mixA_scr", [NT_OWN, 128, 1024], F32, kind=skind).ap()
    x1_scr = nc.dram_tensor("x1_scr", [NT_OWN, 128, 1024], F32, kind=skind).ap()
    h2T_scr = nc.dram_tensor("h2T_scr", [NT_OWN, 128, 1024], BF16, kind=skind).ap()
    dbgt = {}
    if dbg:
        dbgt["og"] = nc.dram_tensor("og_dbg", [NT_OWN, 128, 1024], BF16, kind="ExternalOutput").ap()
        dbgt["onsa"] = nc.dram_tensor("onsa_dbg", [NT_OWN, 128, 512], BF16, kind="ExternalOutput").ap()

    P = Prog(nc)
    finals = []

    def ACT(R, W, **kw):
        return P.op("act", lambda e: e.activation(**kw), R, W)

    def MM(R, W, out, lhsT, rhs, start, stop):
        return P.op("pe", lambda e: e.matmul(out, lhsT=lhsT, rhs=rhs, start=start, stop=stop,
                                             skip_group_check=True), R, W)

    def TR(R, W, out, in_, ident):
        return P.op("pe", lambda e: e.transpose(out, in_, ident), R, W)

    def TT(eng, R, W, out, in0, in1, op):
        return P.op(eng, lambda e: e.tensor_tensor(out=out, in0=in0, in1=in1, op=op), R, W)

    def TS(eng, R, W, out, in0, s1, s2, op0, op1=None):
        if op1 is None:
            return P.op(eng, lambda e: e.tensor_scalar(out=out, in0=in0, scalar1=s1, scalar2=None, op0=op0), R, W)
        return P.op(eng, lambda e: e.tensor_scalar(out=out, in0=in0, scalar1=s1, scalar2=s2, op0=op0, op1=op1), R, W)

    def STT(R, W, out, in0, scalar, in1, op0, op1):
        return P.op("dve", lambda e: e.scalar_tensor_tensor(out=out, in0=in0, scalar=scalar, in1=in1,
                                                            op0=op0, op1=op1), R, W)

    def CP(eng, R, W, out, in_):
        if eng == "act":
            return P.op("act", lambda e: e.copy(out=out, in_=in_), R, W)
        return P.op(eng, lambda e: e.tensor_copy(out=out, in_=in_), R, W)

    def RECIP(R, W, out, in_):
        return P.op("dve", lambda e: e.reciprocal(out=out, in_=in_), R, W)

    def MEMSET(eng, W, ap, val):
        return P.op(eng, lambda e: e.memset(ap, val), (), W)

    def wload(dst, src, key):
        n = dst.shape[-1]
        if n > 2048:
            assert len(dst.shape) == 2
            r = None
            for c0 in range(0, n, 2048):
                r = P.dma("pool", dst[:, c0:c0 + 2048], src[:, c0:c0 + 2048], W=[key + f"_{c0}"])
            P.alias[key] = [key + f"_{c0}" for c0 in range(0, n, 2048)]
            return r
        return P.dma("pool", dst, src, W=[key])

    with contextlib.ExitStack() as gst:
        def gsb(name, shape, dt):
            return gst.enter_context(nc.sbuf_tensor("sb_" + name, list(shape), dt))
        Fb = [gst.enter_context(nc.psum_tensor(f"F{i}", [128, 512], F32)) for i in range(8)]
        FK = [f"F{i}" for i in range(8)]
        P.excl = set(FK)
        Fbf = [f[:].bitcast(BF16) for f in Fb]
        id16 = gsb("id16", [128, 128], BF16)
        wload(id16[:], I["ident"], "id16")
        wl_dummy = gsb("wl_dummy", [128, 1], F32)
        nhalf = gsb("nhalf", [128, 4], F32)
        MEMSET("pool", ["nhalf"], nhalf[:], -0.5)

        def rstd_pow(ss, tmp, rstd, mul, eps, key, ncol):
            TS("dve", [key + "ss"], [key + "t"], tmp, ss, mul, eps, ALU.mult, ALU.add)
            return P.op("pool", lambda e: e.tensor_tensor(out=rstd, in0=tmp, in1=nhalf[:, 0:ncol], op=ALU.pow),
                        [key + "t", "nhalf"], [key + "rstd"])

        wkv = gsb("wkv", [128, 8, 768], BF16)
        stW1 = contextlib.ExitStack()
        sbw = lambda name, shape, dt: stW1.enter_context(nc.sbuf_tensor("sb_" + name, list(shape), dt))
        wk = sbw("wk", [128, 8, 512], BF16)
        wv = sbw("wv", [128, 8, 1024], BF16)
        wq = sbw("wq", [128, 8, 512], BF16)
        wr = sbw("wr", [128, 8, 1024], BF16)
        wv4g = lambda c0, c1: I["w_in"][:, c0:c1].rearrange("(kc p) c -> p kc c", p=128)

        with contextlib.ExitStack() as st:
            sb = lambda name, shape, dt: st.enter_context(nc.sbuf_tensor("sb_" + name, list(shape), dt))
            NS = 3
            NX = 5
            xt = [sb(f"p0xt{i}", [128, 1024], F32) for i in range(NX)]
            hb = [sb(f"p0hb{i}", [128, 1024], BF16) for i in range(NS)]
            hTs = [sb(f"p0hT{i}", [128, 8, 128], BF16) for i in range(NS)]
            ss0 = [sb(f"p0ss{i}", [128, 1], F32) for i in range(NS)]
            tm0 = [sb(f"p0tm{i}", [128, 1], F32) for i in range(NS)]
            rs0 = [sb(f"p0rs{i}", [128, 1], F32) for i in range(NS)]
            st6 = [sb(f"p0st6{i}", [128, 12], F32) for i in range(NS)]
            mv0 = [sb(f"p0mv{i}", [128, 2], F32) for i in range(NS)]
            aTs = [sb(f"p0aT{i}", [17, 128], BF16) for i in range(2)]
            ezs = [sb(f"p0ez{i}", [128, 512], F32) for i in range(2)]
            L32 = [sb(f"p0L{i}", [128, 512], F32) for i in range(2)]
            Eb = [sb(f"p0E{i}", [128, 512], F32) for i in range(2)]
            junk0 = sb("p0junk", [128, 1024], BF16)
            gb = sb("p0gb", [128, 1024], F32)
            wa = sb("p0wa", [128, 8, 16], BF16)
            walpha = sb("p0walpha", [17, 512], BF16)
            tril = sb("p0tril", [128, 128], F32)
            P.dma("act", gb[:], I["gpre_b"], W=["gb0"])
            P.dma("act", tril[:], I["tril"], W=["tril"])
            wload(wa[:], I["w_in"][:, 3072:3088].rearrange("(kc p) c -> p kc c", p=128), "wa")
            wload(walpha[:], I["walpha"], "walpha")
            for i in range(2):
                MEMSET("dve", [f"aT{i}"], aTs[i][:], 1.0)
            wload(wk[:], wv4g(512, 1024), "wk")
            wload(wv[:], wv4g(1024, 2048), "wv")
            wload(wq[:], wv4g(0, 512), "wq")
            wload(wr[:], wv4g(2048, 3072), "wr")
            H = lambda hd: slice(hd * 128, (hd + 1) * 128)

            def xload(n):
                P.dma("sp", xt[n % NX][:], I["xc"][n * 128:(n + 1) * 128, :], W=[f"xt{n % NX}"])

            def s1a(n):
                s = n % NS
                x_ = n % NX
                if n + 2 < NT_CTX:
                    xload(n + 2)
                for c in range(2):
                    P.op("dve", lambda e, o_=st6[s][:, c * 6:(c + 1) * 6], i_=xt[x_][:, c * 512:(c + 1) * 512]: e.bn_stats(out=o_, in_=i_), [f"xt{x_}"], [f"st6{s}"])
                P.op("dve", lambda e, o_=mv0[s][:], i_=st6[s][:]: e.bn_aggr(out=o_, in_=i_), [f"st6{s}"], [f"mv{s}"])
                STT([f"mv{s}"], [f"n0{s}ss"], ss0[s][:], mv0[s][:, 0:1], mv0[s][:, 0:1], mv0[s][:, 1:2], ALU.mult, ALU.add)
                rstd_pow(ss0[s][:], tm0[s][:], rs0[s][:], 1.0, EPS, f"n0{s}", 1)

            def s1b(n):
                s = n % NS
                a2 = n % 2
                ACT([f"xt{n % NX}", f"n0{s}rstd"], [f"hb{s}"], out=hb[s][:], in_=xt[n % NX][:], func=AF.Identity, scale=rs0[s][:, 0:1])
                bank = 6 + a2
                for kc in range(8):
                    TR([f"hb{s}", "id16"], [FK[bank]], Fbf[bank][:, kc * 128:(kc + 1) * 128], hb[s][:, kc * 128:(kc + 1) * 128], id16[:])
                TT("dve", [FK[bank], "gb0"], [f"hTs{s}"], hTs[s][:].rearrange("p k t -> p (k t)"), Fbf[bank][:, 0:1024], gb[:], ALU.mult)
                P.dma("sp", hT_scr[n], hTs[s][:].rearrange("p k t -> p (k t)"), R=[f"hTs{s}"], W=[("hT", n)])
                for kc in range(8):
                    MM([f"hTs{s}", "wa"], [FK[a2]], Fb[a2][0:16, 0:128], wa[:, kc, :], hTs[s][:, kc, :], kc == 0, kc == 7)
                CP("dve", [FK[a2]], [f"aT{a2}"], aTs[a2][0:16, :], Fb[a2][0:16, 0:128])
                MM([f"aT{a2}", "walpha"], [FK[2 + a2]], Fb[2 + a2][:, :], aTs[a2][:, :], walpha[:, :], True, True)

            def s2(n):
                a2 = n % 2
                ACT([FK[2 + a2]], [f"ez{a2}"], out=ezs[a2][:], in_=Fb[2 + a2][:, :], func=AF.Exp, scale=-1.0)
                ACT([f"ez{a2}"], [f"L{a2}"], out=L32[a2][:], in_=ezs[a2][:], func=AF.Ln, bias=1.0)
                for hd in range(4):
                    MM([f"L{a2}", "tril"], [FK[4 + a2]], Fb[4 + a2][:, H(hd)], L32[a2][:, H(hd)], tril[:, :], True, True)

            def s3(n):
                a2 = n % 2
                CP("act", [FK[4 + a2]], [f"E{a2}"], Eb[a2][:], Fb[4 + a2][:, :])
                P.dma("sp", E_scr[n], Eb[a2][:], R=[f"E{a2}"], W=[("E", n)])

            xload(0)
            xload(1)
            for k in range(NT_CTX + 3):
                if k < NT_CTX:
                    s1a(k)
                if 0 <= k - 1 < NT_CTX:
                    s1b(k - 1)
                if 0 <= k - 2 < NT_CTX:
                    s2(k - 2)
                if 0 <= k - 3 < NT_CTX:
                    s3(k - 3)
        P.fence()

        if stop_after >= 1:
            with contextlib.ExitStack() as st:
                sb = lambda name, shape, dt: st.enter_context(nc.sbuf_tensor("sb_" + name, list(shape), dt))
                wma = sb("wma", [128, 8, 1024], BF16)
                wpg = sb("wpgs", [128, 8, 1024], BF16)
                win = I["w_in"]
                wv4 = lambda c0, c1: win[:, c0:c1].rearrange("(kc p) c -> p kc c", p=128)
                wload(wma[:], wv4(4392, 5416), "wma")
                wload(wpg[:], I["wpg"].rearrange("(kc p) c -> p kc c", p=128), "wpg")
                wload(wkv[:], I["w_in"][:, 3600:4368].rearrange("(kc p) c -> p kc c", p=128), "wkv")
                gmask = sb("gmask", [128, 512], F32)
                P.dma("act", gmask[:], I["gmask"], W=["gmask"])
                ngrep = sb("ngrep", [128, 1024], F32)
                P.dma("act", ngrep[:], I["ngrep"], W=["ngrep"])
                NS = 3
                hTt = [sb(f"p1hT{i}", [128, 8, 128], BF16) for i in range(NS)]
                Et = [sb(f"p1E{i}", [128, 1024], F32) for i in range(NS)]
                BTt = [sb(f"p1BT{i}", [128, 512], F32) for i in range(NS)]
                keT = [sb(f"keT{i}", [128, 512], BF16) for i in range(2)]
                qeT = [sb(f"qeT{i}", [128, 512], BF16) for i in range(2)]
                ketm = [sb(f"ketm{i}", [128, 512], BF16) for i in range(2)]
                attm = [sb(f"attm{i}", [128, 512], BF16) for i in range(2)]
                v16 = [sb(f"v16{i}", [128, 1024], BF16) for i in range(2)]
                S32 = sb("S32", [128, 1024], F32)
                S16 = sb("S16", [128, 1024], BF16)
                MEMSET("dve", ["S32"], S32[:], 0.0)
                MEMSET("pool", ["S16"], S16[:], 0.0)
                tr = sb("tr", [128, 1024], F32)
                sr = sb("sr", [128, 1024], F32)
                ngsr = [sb(f"ngsr{i}", [128, 1024], F32) for i in range(2)]
                ss4 = [sb(f"ss4{i}", [128, 4], F32) for i in range(2)]
                t4 = [sb(f"t4{i}", [128, 4], F32) for i in range(2)]
                rstd4 = [sb(f"rstd4{i}", [128, 4], F32) for i in range(2)]
                junk1 = sb("junk1", [128, 256], BF16)
                og = [sb(f"og{i}", [128, 1024], BF16) for i in range(2)]
                ogT = sb("ogT", [128, 8, 128], BF16)
                tga = sb("tga", [128, 1024], F32)
                mixA = [sb(f"mixA{i}", [128, 1024], F32) for i in range(2)]
                V = lambda hd: slice(hd * 256, (hd + 1) * 256)
                O0 = NT_CTX - NT_OWN

                def loads(n):
                    s = n % NS
                    P.dma("sp", hTt[s][:], hT_scr[n].rearrange("p (k t) -> p k t", t=128), R=[("hT", n)], W=[f"p1hT{s}"])
                    P.dma("sp", BTt[s][:], E_scr[n], R=[("E", n)], W=[f"p1BT{s}"])

                def y1(n):
                    s = n % NS
                    d = n % 2
                    hk = f"p1hT{s}"
                    ht = hTt[s]
                    for kc in range(8):
                        TR([f"og{d}", "id16"], [FK[5]], Fbf[5][:, kc * 128:(kc + 1) * 128], og[d][:, kc * 128:(kc + 1) * 128], id16[:])
                    for hf in range(2):
                        for kc in range(8):
                            MM([hk, "wma"], [FK[6 + hf]], Fb[6 + hf][:, :], ht[:, kc, :], wma[:, kc, hf * 512:(hf + 1) * 512], kc == 0, kc == 7)
                    CP("act", [FK[5]], ["ogT"], ogT[:].rearrange("p k t -> p (k t)"), Fbf[5][:, 0:1024])
                    for hf in range(2):
                        ACT([FK[6 + hf]], ["tga"], out=tga[:, hf * 512:(hf + 1) * 512], in_=Fb[6 + hf][:, :], func=AF.Tanh, scale=0.5)

                def y2(n):
                    d = n % 2
                    i_own = n - O0
                    for hf in range(2):
                        hs = slice(hf * 512, (hf + 1) * 512)
                        for kc in range(8):
                            MM(["ogT", "wpg"], [FK[6 + hf]], Fb[6 + hf][:, :], ogT[:, kc, :], wpg[:, kc, hs], kc == 0, kc == 7)
                        STT(["tga", FK[6 + hf]], [f"mixA{d}"], mixA[d][:, hs], tga[:, hs], 1.0, Fb[6 + hf][:, :], ALU.add, ALU.mult)
                    f = P.dma("sp", mixA_scr[i_own], mixA[d][:], R=[f"mixA{d}"], W=[("mixA", i_own)])
                    if dbg:
                        finals.append(f)

                def stage(n):
                    s = n % NS
                    d = n % 2
                    own = n >= O0
                    pown = n - 1 >= O0
                    hk, ek = f"p1hT{s}", f"p1E{s}"
                    ht = hTt[s]
                    E1, E2 = Et[s][:, 0:512], Et[s][:, 512:1024]
                    ACT([f"p1BT{s}"], [ek], out=E1, in_=BTt[s][:], func=AF.Exp, scale=-1.0 / 16.0)
                    ACT([f"p1BT{s}"], [ek], out=E2, in_=BTt[s][:], func=AF.Exp, scale=1.0 / 16.0)
                    oc_ = lambda hd: slice((hd % 2) * 256, (hd % 2) * 256 + 256)
                    for hd in range(4):
                        for kc in range(8):
                            MM([hk, "wk"], [FK[0]], Fb[0][:, H(hd)], wk[:, kc, H(hd)], ht[:, kc, :], kc == 0, kc == 7)
                    if own:
                        for hd in range(4):
                            for kc in range(8):
                                MM([hk, "wq"], [FK[1]], Fb[1][:, H(hd)], wq[:, kc, H(hd)], ht[:, kc, :], kc == 0, kc == 7)
                        for hf in range(2):
                            for kc in range(8):
                                MM([hk, "wr"], [FK[6 + hf]], Fb[6 + hf][:, :], ht[:, kc, :], wr[:, kc, hf * 512:(hf + 1) * 512], kc == 0, kc == 7)
                    for hf in range(2):
                        for kc in range(8):
                            MM([hk, "wv"], [FK[2 + hf]], Fb[2 + hf][:, :], ht[:, kc, :], wv[:, kc, hf * 512:(hf + 1) * 512], kc == 0, kc == 7)
                    TT("dve", [FK[0], ek], [f"keT{d}"], keT[d][:], Fb[0][:, :], E2, ALU.mult)
                    if own:
                        STT([FK[1], ek], [f"qeT{d}"], qeT[d][:], Fb[1][:, :], 128.0 ** -0.5, E1, ALU.mult, ALU.mult)
                        for hf in range(2):
                            ACT([FK[6 + hf]], ["tr"], out=tr[:, hf * 512:(hf + 1) * 512], in_=Fb[6 + hf][:, :], func=AF.Tanh, scale=0.5)
                    for hf in range(2):
                        CP("act", [FK[2 + hf]], [f"v16{d}"], v16[d][:, hf * 512:(hf + 1) * 512], Fb[2 + hf][:, :])
                    if own:
                        for hf in range(2):
                            hs = slice(hf * 512, (hf + 1) * 512)
                            STT(["tr", FK[6 + hf]], ["sr"], sr[:, hs], tr[:, hs], 1.0, Fb[6 + hf][:, :], ALU.add, ALU.mult)
                        TT("pool", ["sr", "ngrep"], [f"ngsr{d}"], ngsr[d][:], sr[:], ngrep[:], ALU.mult)
                    for hd in range(4):
                        TR([f"keT{d}", "id16"], [FK[5]], Fbf[5][:, H(hd)], keT[d][:, H(hd)], id16[:])
                    if own:
                        for hd in range(4):
                            MM([f"keT{d}", f"qeT{d}"], [FK[0]], Fb[0][:, H(hd)], keT[d][:, H(hd)], qeT[d][:, H(hd)], True, True)
                    CP("act", [FK[5]], [f"ketm{d}"], ketm[d][:], Fbf[5][:, 0:512])
                    if own:
                        TT("dve", [FK[0], "gmask"], [f"attm{d}"], attm[d][:], Fb[0][:, :], gmask[:], ALU.mult)
                    if pown:
                        y1(n - 1)
                    for hd in range(4):
                        bk = (1, 4)[hd // 2]
                        MM([f"ketm{d}", f"v16{d}"], [FK[bk]], Fb[bk][:, oc_(hd)], ketm[d][:, H(hd)], v16[d][:, V(hd)], True, True)
                    if own:
                        for hd in range(4):
                            bk = 2 + hd // 2
                            MM([f"attm{d}", f"v16{d}"], [FK[bk]], Fb[bk][:, oc_(hd)], attm[d][:, H(hd)], v16[d][:, V(hd)], True, False)
                            MM([f"qeT{d}", "S16"], [FK[bk]], Fb[bk][:, oc_(hd)], qeT[d][:, H(hd)], S16[:, V(hd)], False, True)
                    pass
                    for hd in range(4):
                        bk = (1, 4)[hd // 2]
                        dec = Et[s][:, hd * 128 + 127:hd * 128 + 128]
                        TS("dve", ["S32", ek], ["S32"], S32[:, V(hd)], S32[:, V(hd)], dec, None, ALU.mult)
                        STT([FK[bk], ek, "S32"], ["S32"], S32[:, V(hd)], Fb[bk][:, oc_(hd)], dec, S32[:, V(hd)], ALU.mult, ALU.add)
                    if own:
                        for hd in range(4):
                            bk = 2 + hd // 2
                            ACT([FK[bk]], ["junk1", f"g4{d}ss"], out=junk1[:], in_=Fb[bk][:, oc_(hd)], func=AF.Square, accum_out=ss4[d][:, hd:hd + 1])
                        rstd_pow(ss4[d][:], t4[d][:], rstd4[d][:], 4.0 / 256.0, 4.0 * EPS, f"g4{d}", 4)
                        for hd in range(4):
                            bk = 2 + hd // 2
                            STT([FK[bk], f"g4{d}rstd", f"ngsr{d}"], [f"og{d}"], og[d][:, V(hd)], Fb[bk][:, oc_(hd)], rstd4[d][:, hd:hd + 1], ngsr[d][:, V(hd)], ALU.mult, ALU.mult)
                        if dbg:
                            finals.append(P.dma("sp", dbgt["og"][n - O0], og[d][:], R=[f"og{d}"]))
                    CP("act", ["S32"], ["S16"], S16[:], S32[:])
                    if pown:
                        y2(n - 1)

                loads(0)
                loads(1)
                for n in range(NT_CTX):
                    stage(n)
                    if n + 2 < NT_CTX:
                        loads(n + 2)
                y1(NT_CTX - 1)
                y2(NT_CTX - 1)
            P.fence()
        stW1.close()

        if stop_after >= 2:
            with contextlib.ExitStack() as st23:
                sb23 = lambda name, shape, dt: st23.enter_context(nc.sbuf_tensor("sb_" + name, list(shape), dt))
                KT = sb23("KT", [128, 4, 4096], BF16)
                VA = sb23("VA", [128, NT_CTX, 4, 65], BF16)
                Kc = sb23("Kc", [128, 2, 256], BF16)
                Vc = sb23("Vc", [128, 2, 2, 128], BF16)
                MEMSET("dve", ["VA"], VA[:], 1.0)
                MEMSET("pool", ["Vc"] + [f"Vco{g}{nt}" for g in range(2) for nt in range(2)], Vc[:], 1.0)
                KTK = ["KT"] + [f"KT{c}{v}" for c in "eaw" for v in range(2)]
                KCK = ["Kc", "Kca0", "Kca1"]
                VCK = ["Vc"] + [f"Vco{g}{nt}" for g in range(2) for nt in range(2)]
                wnq = sb23("wnq", [128, 8, 512], BF16)
                wg = sb23("wg", [128, 8, 24], BF16)
                wmb = sb23("wmb", [128, 8, 1024], BF16)
                wpn = sb23("wpns", [128, 4, 1024], BF16)
                wo = sb23("wos", [128, 8, 1024], BF16)
                wv4 = lambda c0, c1: I["w_in"][:, c0:c1].rearrange("(kc p) c -> p kc c", p=128)
                trib = sb23("trib", [128, 512], BF16)
                trilo = sb23("trilo", [128, 512], BF16)
                tcs = sb23("tcs", [16, 512], BF16)
                gpost = sb23("gpost", [128, 1024], F32)
                gffn = sb23("gffn", [128, 1024], F32)
                with contextlib.ExitStack() as st:
                    sb = lambda name, shape, dt: st.enter_context(nc.sbuf_tensor("sb_" + name, list(shape), dt))
                    kcT = sb("kcT", [128, 4096], BF16)
                    vcT = sb("vcT", [128, 4096], BF16)
                    w1 = {"k": sb("w1k", [128, 32, 64], BF16), "v": sb("w1v", [128, 32, 64], BF16)}
                    w2 = {"k": sb("w2k", [64, 64], BF16), "v": sb("w2v", [64, 64], BF16)}
                    pe = {"k": sb("pek", [128, 32, 2], BF16), "v": sb("pev", [128, 32, 2], BF16)}
                    for kv in "kv":
                        for hh in range(2):
                            wload(w1[kv][hh * 64:(hh + 1) * 64, :, :].rearrange("p l j -> p (l j)"), I["w1" + kv], "w1" + kv + str(hh))
                        wload(w2[kv][:], I["w2" + kv], "w2" + kv)
                        wload(pe[kv][:], I["pe" + kv], "pe" + kv)
                    for v in range(2):
                        wload(KT[64:123, v, :], I["ekt"], f"KTe{v}")
                        wload(KT[123:128, v, :], I["kaug"], f"KTa{v}")
                        wload(KT[64:69, 2 + v, :], I["kaug"], f"KTw{v}")
                    for g in range(2):
                        wload(Kc[64:69, g, :], I["caug"], f"Kca{g}")
                        for nt in range(2):
                            wload(Vc[:, g, nt, 64:128], I["ov"][:, nt, :], f"Vco{g}{nt}")
                    wload(wnq[:], wv4(3088, 3600), "wnq")
                    wload(wg[:], wv4(4368, 4392), "wg")
                    wload(wmb[:], wv4(5416, 6440), "wmb")
                    wload(wpn[:], I["wpn"].rearrange("(kc p) c -> p kc c", p=128), "wpn")
                    wload(wo[:], I["wo"].rearrange("(kc p) c -> p kc c", p=128), "wo")
                    wload(trib[:], I["trib"], "trib")
                    wload(trilo[:], I["trilo"], "trilo")
                    wload(tcs[:], I["tc"], "tcs")
                    P.dma("act", gpost[:], I["gpost_rep"], W=["gpost"])
                    P.dma("act", gffn[:], I["gffn_b"], W=["gffn"])
                    h1 = sb("h1", [64, 256], BF16)
                    MEMSET("dve", ["h1"], h1[:], 0.0)
                    csb = sb("csb", [64, 1], F32)
                    NS = 3
                    hTt = [sb(f"p2hT{i}", [128, 8, 128], BF16) for i in range(NS)]

                    def loads2(n):
                        s = n % NS
                        P.dma("sp", hTt[s][:], hT_scr[n].rearrange("p (k t) -> p k t", t=128), R=[("hT", n)], W=[f"p2hT{s}"])
                    loads2(0)
                    loads2(1)
                    for n in range(NT_CTX):
                        if n + 2 < NT_CTX:
                            loads2(n + 2)
                        s = n % NS
                        hk = f"p2hT{s}"
                        ht = hTt[s]
                        tok = slice(n * 128, (n + 1) * 128)
                        b0 = 3 * (n % 2)
                        for j, dst in enumerate((kcT, vcT)):
                            for kc in range(8):
                                MM([hk, "wkv"], [FK[b0]], Fb[b0][:, j * 128:(j + 1) * 128], wkv[:, kc, j * 128:(j + 1) * 128], ht[:, kc, :], kc == 0, kc == 7)
                        CP("act", [FK[b0]], ["kcT"], kcT[:, tok], Fb[b0][:, 0:128])
                        CP("act", [FK[b0]], ["vcT"], vcT[:, tok], Fb[b0][:, 128:256])
                        for v in range(4):
                            c0 = (256, 320, 512, 576)[v]
                            for kc in range(8):
                                MM([hk, "wkv"], [FK[b0 + 1]], Fb[b0 + 1][0:64, v * 128:(v + 1) * 128], wkv[:, kc, c0:c0 + 64], ht[:, kc, :], kc == 0, kc == 7)
                        CP("dve", [FK[b0 + 1]], ["KT"], KT[0:64, :, tok], Fb[b0 + 1][0:64, :].rearrange("p (v t) -> p v t", t=128))
                        for j, c0 in enumerate((384, 640)):
                            for kc in range(8):
                                MM([hk, "wkv"], [FK[b0 + 2]], Fb[b0 + 2][:, j * 128:(j + 1) * 128], ht[:, kc, :], wkv[:, kc, c0:c0 + 128], kc == 0, kc == 7)
                        CP("dve", [FK[b0 + 2]], ["VA"], VA[:, n, :, 0:64], Fb[b0 + 2][:, 0:256].rearrange("p (v d) -> p v d", d=64))
                    for g in range(2):
                        gs = slice(g * 64, (g + 1) * 64)
                        for kv, src in (("k", kcT), ("v", vcT)):
                            bk = 6 if kv == "k" else 7
                            view = src[:].rearrange("p (n s) -> p n s", s=16)
                            for l in range(32):
                                MM([kv + "cT", "w1" + kv + "0", "w1" + kv + "1"], [FK[bk]], Fb[bk][0:64, 0:255], w1[kv][gs, l, :],
                                   view[gs, (l // 16):(l // 16) + 255, l % 16], l == 0, False)
                                MM(["pe" + kv, "w1" + kv + "0", "w1" + kv + "1"], [FK[bk]], Fb[bk][0:64, 256:258], w1[kv][gs, l, :],
                                   pe[kv][gs, l, :], False, l == 31)
                            CP("dve", [FK[bk]], ["csb"], csb[:], Fb[bk][0:64, 256:257])
                            ACT([FK[bk], "csb"], ["h1"], out=h1[:, 0:255], in_=Fb[bk][0:64, 0:255], func=AF.Silu, bias=csb[:, 0:1])
                            if kv == "k":
                                MM(["h1", "w2k"], [FK[5]], Fb[5][0:64, 0:256], w2["k"][:, :], h1[:, :], True, True)
                                CP("dve", [FK[5]], ["Kc"], Kc[0:64, g, :], Fb[5][0:64, 0:256])
                            else:
                                for nt in range(2):
                                    MM(["h1", "w2v"], [FK[5]], Fb[5][:, nt * 64:(nt + 1) * 64], h1[:, nt * 128:(nt + 1) * 128], w2["v"][:, :], True, True)
                                CP("dve", [FK[5]], ["Vc"], Vc[:, g, :, 0:64], Fb[5][:, 0:128].rearrange("p (t d) -> p t d", d=64))
                P.fence()

                if stop_after >= 3:
                    with contextlib.ExitStack() as st:
                        sb = lambda name, shape, dt: st.enter_context(nc.sbuf_tensor("sb_" + name, list(shape), dt))
                        NS = 3
                        shc = [sb(f"shc{i}", [16, 256], BF16) for i in range(NS)]
                        selA = [sb(f"selA{i}", [128, 64], F32) for i in range(NS)]
                        selB = [sb(f"selB{i}", [128, 64], F32) for i in range(NS)]
                        Qw = [sb(f"Qw{i}", [128, 2, 512], BF16) for i in range(NS)]
                        Qs = [sb(f"Qs{i}", [128, 2, 512], BF16) for i in range(NS)]
                        QsB = [sb(f"QsB{i}", [128, 2, 512], BF16) for i in range(2)]
                        for i in range(NS):
                            MEMSET("dve", [f"Qs{i}sel0", f"Qs{i}sel1", f"Qs{i}q", f"Qs{i}aug"], Qs[i][:], 0.0)
                        for i in range(2):
                            MEMSET("pool", [f"QsB{i}sel0", f"QsB{i}sel1", f"QsB{i}q", f"QsB{i}aug"], QsB[i][:], 0.0)
                        selE = [[sb(f"selE{g}{v}", [128, 128], BF16) for v in range(2)] for g in range(2)]
                        for g in range(2):
                            for v in range(2):
                                MEMSET("pool", [f"selE{g}{v}"], selE[g][v][:], 0.0)
                        hTt = [sb(f"p3hT{i}", [128, 8, 128], BF16) for i in range(NS)]
                        NPE = 4
                        Pe = [sb(f"Pe{i}", [128, 512], BF16) for i in range(NPE)]
                        tg_ = sb("tg_", [128, 24], F32)
                        gsig = [sb(f"gsig{i}", [128, 24], F32) for i in range(2)]
                        tmg = sb("tmg", [128, 1024], F32)
                        pgin = [sb(f"pgin{i}", [128, 1024], F32) for i in range(2)]
                        xt = [sb(f"p3xt{i}", [128, 1024], F32) for i in range(2)]
                        x1 = [sb(f"x1_{i}", [128, 1024], F32) for i in range(2)]
                        h2T = [sb(f"h2T{i}", [128, 1024], BF16) for i in range(2)]
                        hb2 = sb("hb2", [128, 1024], BF16)
                        tmp1 = sb("tmp1", [128, 1024], F32)
                        mixed = sb("mixed", [128, 1024], BF16)
                        mixT = sb("mixT", [128, 8, 128], BF16)
                        onsa = [sb(f"onsa{i}", [128, 512], BF16) for i in range(2)]
                        onT = sb("onT", [128, 4, 128], BF16)
                        denc = sb("denc", [128, 4], F32)
                        rdc = sb("rdc", [128, 4], F32)
                        rdw = sb("rdw", [128, 4], F32)
                        rds = sb("rds", [128, 4], F32)
                        cfc = sb("cfc", [128, 4], F32)
                        cfw = sb("cfw", [128, 4], F32)
                        cfs = sb("cfs", [128, 4], F32)
                        ocs = [sb(f"ocs{g}", [128, 4, 64], F32) for g in range(2)]
                        ows = [sb(f"ows{g}", [128, 4, 64], F32) for g in range(2)]
                        tmpi = sb("tmpi", [128, 4, 64], F32)
                        tmpo = sb("tmpo", [128, 4, 64], F32)
                        imp = sb("imp", [128, 64], F32)
                        sc1 = sb("sc1", [128, 64], F32)
                        sc2 = sb("sc2", [128, 64], F32)
                        m8 = sb("m8", [128, 16], F32)
                        ssy = sb("ssy", [128, 2], F32)
                        ssy1 = sb("ssy1", [128, 1], F32)
                        ty = sb("ty", [128, 1], F32)
                        rstdy = sb("rstdy", [128, 1], F32)
                        ssh = sb("ssh", [128, 1], F32)
                        st6h = sb("st6h", [128, 12], F32)
                        mvh = sb("mvh", [128, 2], F32)
                        th = sb("th", [128, 1], F32)
                        rstdh = sb("rstdh", [128, 1], F32)
                        junk3 = sb("junk3", [128, 1024], BF16)
                        O0 = NT_CTX - NT_OWN
                        pe_rr = [0]
                        sc_rr = [0]
                        pending = []

                        class Job:
                            def __init__(self, score, pv, pre=None, after=None):
                                self.score, self.pv, self.pre, self.after = score, pv, pre, after
                                self.k = None

                        def emit_pv(jb):
                            for (out_ap, bkey, r, rhs_ap, rkey, st_, sp_) in jb.pv:
                                MM([f"Pe{jb.k}"] + (VCK if rkey == "Vcall" else [rkey]), [bkey], out_ap, Pe[jb.k][:, r * 128:(r + 1) * 128], rhs_ap, st_, sp_)
                            if jb.after is not None:
                                jb.after()

                        SCB = (0, 1, 3)

                        def run_jobs(jobs):
                            q_ = []
                            L0, J0 = len(pending), len(jobs)
                            for ji, jb in enumerate(jobs):
                                if jb.pre is not None:
                                    jb.pre()
                                b = SCB[sc_rr[0] % 3]
                                sc_rr[0] += 1
                                ns = len(jb.score)
                                for idx, (lhsT, rhs, R) in enumerate(jb.score):
                                    MM(R, [FK[b]], Fb[b][:, :], lhsT, rhs, idx == 0, idx == ns - 1)
                                if len(q_) >= 2:
                                    emit_pv(q_.pop(0))
                                k = pe_rr[0] % NPE
                                pe_rr[0] += 1
                                ACT([FK[b]], [f"Pe{k}"], out=Pe[k][:], in_=Fb[b][:, :], func=AF.Exp)
                                jb.k = k
                                q_.append(jb)
                                for _ in range(((ji + 1) * L0) // J0 - (ji * L0) // J0):
                                    pending.pop(0)()
                            while q_:
                                emit_pv(q_.pop(0))

                        def loads3(i):
                            s = i % NS
                            sB = i % 2
                            m = O0 + i
                            P.dma("sp", hTt[s][:], hT_scr[m].rearrange("p (k t) -> p k t", t=128), R=[("hT", m)], W=[f"p3hT{s}"])
                            P.dma("sp", selA[s][:], I["selA"][i], W=[f"selA{s}"])
                            P.dma("sp", selB[s][:], I["selB"][i], W=[f"selB{s}"])
                            wload(shc[s][:], I["shc"][i], f"shc{s}")
                            for g in range(2):
                                wload(Qw[s][64:69, g, :], I["qaug"][g, i], f"Qw{s}aug")
                                wload(Qs[s][123:128, g, :], I["qaug"][g, i], f"Qs{s}aug")
                                if m >= 29:
                                    wload(QsB[sB][123:128, g, :], I["qaug"][g, i], f"QsB{sB}aug")

                        def tile(i):
                            s = i % NS
                            sB = i % 2
                            d = i % 2
                            m = O0 + i
                            useB = m >= 29
                            hk = f"p3hT{s}"
                            ht = hTt[s]
                            P.dma("sp", xt[d][:], I["xc"][m * 128:(m + 1) * 128, :], W=[f"p3xt{d}"])
                            P.dma("sp", pgin[d][:], mixA_scr[i], R=[("mixA", i)], W=[f"pgin{d}"])
                            for g in range(2):
                                for r in range(4):
                                    h = g * 4 + r
                                    for kc in range(8):
                                        MM([hk, "wnq"], [FK[g]], Fb[g][0:64, r * 128:(r + 1) * 128], wnq[:, kc, h * 64:(h + 1) * 64], ht[:, kc, :], kc == 0, kc == 7)
                                ACT([FK[g]], [f"Qw{s}q"], out=Qw[s][0:64, g, :], in_=Fb[g][0:64, :], func=AF.Identity, scale=0.125)
                                TS("dve", [FK[g]], [f"Qs{s}q"], Qs[s][0:64, g, :], Fb[g][0:64, :], 0.125, None, ALU.mult)
                                if useB:
                                    TS("dve", [FK[g]], [f"QsB{sB}q"], QsB[sB][0:64, g, :], Fb[g][0:64, :], 0.125, None, ALU.mult)
                            for kc in range(8):
                                MM([hk, "wg"], [FK[4]], Fb[4][:, 264:288], ht[:, kc, :], wg[:, kc, :], kc == 0, kc == 7)
                            ACT([FK[4]], ["tg_"], out=tg_[:], in_=Fb[4][:, 264:288], func=AF.Tanh, scale=0.5)
                            TS("dve", ["tg_"], [f"gsig{d}"], gsig[d][:], tg_[:], 0.5, 0.5, ALU.mult, ALU.add)
                            gsv = gsig[d][:].rearrange("p (h b) -> p h b", b=3)
                            accCI = Fb[2][:, :].rearrange("p (r c) -> p r c", c=128)
                            accC = accCI[:, :, 0:64]
                            accI = accCI[:, :, 64:128]
                            accS = Fb[4][:, 0:260].rearrange("p (r c) -> p r c", c=65)
                            accW = Fb[5][:, 0:260].rearrange("p (r c) -> p r c", c=65)
                            bc = lambda t: t[:].unsqueeze(2).broadcast_to([128, 4, 64])

                            def after_cmp(g):
                                def f():
                                    P.op("dve", lambda e: e.tensor_reduce(out=denc[:], in_=accI, axis=mybir.AxisListType.X, op=ALU.add), [FK[2]], ["denc"])
                                    TS("dve", ["denc"], ["denc"], denc[:], denc[:], 1e-30, None, ALU.max)
                                    RECIP(["denc"], ["rdc"], rdc[:], denc[:])
                                    TT("dve", [FK[2], "rdc"], ["tmpi"], tmpi[:], accI, bc(rdc), ALU.mult)
                                    P.op("dve", lambda e: e.tensor_reduce(out=imp[:], in_=tmpi[:].rearrange("p r j -> p j r"), axis=mybir.AxisListType.X, op=ALU.add), ["tmpi"], ["imp"])
                                    TT("dve", ["imp", f"selA{s}"], ["sc1"], sc1[:], imp[:], selA[s][:], ALU.max)
                                    TT("dve", ["sc1", f"selB{s}"], ["sc1"], sc1[:], sc1[:], selB[s][:], ALU.min)
                                    P.op("dve", lambda e: e.max(out=m8[:, 0:8], in_=sc1[:]), ["sc1"], ["m8a"])
                                    P.op("dve", lambda e: e.match_replace(out=sc2[:], in_to_replace=m8[:, 0:8], in_values=sc1[:], imm_value=-3.0e38), ["sc1", "m8a"], ["sc2"])
                                    P.op("dve", lambda e: e.max(out=m8[:, 8:16], in_=sc2[:]), ["sc2"], ["m8b"])
                                    TS("dve", ["sc1", "m8b"], [f"selE{g}0"], selE[g][0][:, 64:123], sc1[:, 0:59], m8[:, 15:16], NEGB, ALU.is_lt, ALU.mult)
                                    if useB:
                                        TS("dve", ["sc1", "m8b"], [f"selE{g}1"], selE[g][1][:, 64:123], sc1[:, 5:64], m8[:, 15:16], NEGB, ALU.is_lt, ALU.mult)
                                    TT("dve", ["rdc", f"gsig{d}"], ["cfc"], cfc[:], rdc[:], gsv[:, g * 4:(g + 1) * 4, 0], ALU.mult)
                                    TT("dve", [FK[2], "cfc"], [f"ocs{g}"], ocs[g][:], accC, bc(cfc), ALU.mult)
                                return f

                            def after_win(g):
                                def f():
                                    RECIP([FK[5]], ["rdw"], rdw[:], accW[:, :, 64])
                                    TT("dve", ["rdw", f"gsig{d}"], ["cfw"], cfw[:], rdw[:], gsv[:, g * 4:(g + 1) * 4, 2], ALU.mult)
                                    TT("dve", [FK[5], "cfw"], [f"ows{g}"], ows[g][:], accW[:, :, 0:64], bc(cfw), ALU.mult)
                                return f

                            def pre_sel(g):
                                def f():
                                    TR([f"selE{g}0", "id16"], [FK[2]], Fbf[2][:, 0:128], selE[g][0][:, :], id16[:])
                                    CP("dve", [FK[2]], [f"Qs{s}sel{g}"], Qs[s][64:123, g, :].rearrange("p (r q) -> p r q", q=128),
                                       Fbf[2][64:123, 0:128].unsqueeze(1).broadcast_to([59, 4, 128]))
                                    if useB:
                                        TR([f"selE{g}1", "id16"], [FK[2]], Fbf[2][:, 128:256], selE[g][1][:, :], id16[:])
                                        CP("dve", [FK[2]], [f"QsB{sB}sel{g}"], QsB[sB][64:123, g, :].rearrange("p (r q) -> p r q", q=128),
                                           Fbf[2][64:123, 128:256].unsqueeze(1).broadcast_to([59, 4, 128]))
                                return f

                            def after_sel(g):
                                def f():
                                    RECIP([FK[4]], ["rds"], rds[:], accS[:, :, 64])
                                    TT("dve", ["rds", f"gsig{d}"], ["cfs"], cfs[:], rds[:], gsv[:, g * 4:(g + 1) * 4, 1], ALU.mult)
                                    TT("dve", [FK[4], "cfs"], ["tmpo"], tmpo[:], accS[:, :, 0:64], bc(cfs), ALU.mult)
                                    TT("dve", ["tmpo", f"ocs{g}"], ["tmpo"], tmpo[:], tmpo[:], ocs[g][:], ALU.add)
                                    TT("dve", ["tmpo", f"ows{g}"], [f"onsa{d}"], onsa[d][:, g * 256:(g + 1) * 256].rearrange("p (r c) -> p r c", c=64), tmpo[:], ows[g][:], ALU.add)
                                return f

                            jobs = []
                            qwk = [f"Qw{s}q", f"Qw{s}aug"]
                            for g in range(2):
                                qw = Qw[s][0:69, g, :]
                                for nt in range(2):
                                    score = [(Kc[0:69, g, nt * 128:(nt + 1) * 128], qw, KCK + qwk),
                                             (shc[s][:, nt * 128:(nt + 1) * 128], tcs[:, :], [f"shc{s}", "tcs"])]
                                    pv = []
                                    for r in range(4):
                                        pv.append((Fb[2][:, r * 128:(r + 1) * 128], FK[2], r, Vc[:, g, nt, :], "Vcall", nt == 0 and r == 0, nt == 1 and r == 3))
                                    jobs.append(Job(score, pv, after=after_cmp(g) if nt == 1 else None))
                                for kt in range(m - 4, m + 1):
                                    ks = slice(kt * 128, (kt + 1) * 128)
                                    score = [(KT[0:69, 2 + g, ks], qw, KTK + qwk)]
                                    if kt == m - 4:
                                        score.append((id16[:, :], trilo[:, :], ["id16", "trilo"]))
                                    if kt == m:
                                        score.append((id16[:, :], trib[:, :], ["id16", "trib"]))
                                    pv = [(Fb[5][:, r * 65:(r + 1) * 65], FK[5], r, VA[:, kt, 2 + g, :], "VA", kt == m - 4 and r == 0, kt == m and r == 3) for r in range(4)]
                                    jobs.append(Job(score, pv, after=after_win(g) if kt == m else None))
                            for g in range(2):
                                for kt in range(m + 1):
                                    ks = slice(kt * 128, (kt + 1) * 128)
                                    if kt >= 29:
                                        qa, qk_ = QsB[sB][:, g, :], [f"QsB{sB}q", f"QsB{sB}aug", f"QsB{sB}sel{g}"]
                                    else:
                                        qa, qk_ = Qs[s][:, g, :], [f"Qs{s}q", f"Qs{s}aug", f"Qs{s}sel{g}"]
                                    score = [(KT[:, g, ks], qa, KTK + qk_)]
                                    if kt == m:
                                        score.append((id16[:, :], trib[:, :], ["id16", "trib"]))
                                    pv = [(Fb[4][:, r * 65:(r + 1) * 65], FK[4], r, VA[:, kt, g, :], "VA", kt == 0 and r == 0, kt == m and r == 3) for r in range(4)]
                                    jobs.append(Job(score, pv, pre=pre_sel(g) if kt == 0 else None, after=after_sel(g) if kt == m else None))
                            run_jobs(jobs)

                            def tr_onsa():
                                for kc in range(4):
                                    TR([f"onsa{d}", "id16"], [FK[6]], Fbf[6][:, kc * 128:(kc + 1) * 128], onsa[d][:, kc * 128:(kc + 1) * 128], id16[:])
                                if dbg:
                                    finals.append(P.dma("sp", dbgt["onsa"][i], onsa[d][:], R=[f"onsa{d}"]))

                            def mm_wmb(hf, k0=0, k1=8):
                                hs = slice(hf * 512, (hf + 1) * 512)
                                for kc in range(k0, k1):
                                    MM([hk, "wmb"], [FK[7]], Fb[7][:, :], ht[:, kc, :], wmb[:, kc, hs], kc == 0, kc == 7)

                            def mm_wpn(hf, k0=0, k1=4):
                                hs = slice(hf * 512, (hf + 1) * 512)
                                for kc in range(k0, k1):
                                    MM(["onT", "wpn"], [FK[6]], Fb[6][:, :], onT[:, kc, :], wpn[:, kc, hs], kc == 0, kc == 3)

                            def tanh_mg(hf):
                                hs = slice(hf * 512, (hf + 1) * 512)
                                ACT([FK[7]], ["tmg"], out=tmg[:, hs], in_=Fb[7][:, :], func=AF.Tanh, scale=0.5)

                            def mix(hf):
                                hs = slice(hf * 512, (hf + 1) * 512)
                                STT(["tmg", FK[6]], ["tmp1"], tmp1[:, hs], tmg[:, hs], 1.0, Fb[6][:, :], ALU.add, ALU.mult)
                                TT("dve", ["tmp1", f"pgin{d}"], ["mixed"], mixed[:, hs], tmp1[:, hs], pgin[d][:, hs], ALU.add)

                            def tr_mixed(k0=0, k1=8):
                                for kc in range(k0, k1):
                                    TR(["mixed", "id16"], [FK[7]], Fbf[7][:, kc * 128:(kc + 1) * 128], mixed[:, kc * 128:(kc + 1) * 128], id16[:])

                            def mm_wo(hf, k0, k1):
                                if True:
                                    hs = slice(hf * 512, (hf + 1) * 512)
                                    for kc in range(k0, k1):
                                        MM(["mixT", "wo"], [FK[6 + hf]], Fb[6 + hf][:, :], mixT[:, kc, :], wo[:, kc, hs], kc == 0, kc == 7)

                            def sq_y():
                                for hf in range(2):
                                    ACT([FK[6 + hf]], ["junk3", "yss"], out=junk3[:, 0:512], in_=Fb[6 + hf][:, :], func=AF.Square, accum_out=ssy[:, hf:hf + 1])

                            def x1_step():
                                TT("dve", ["yss"], ["y1ss"], ssy1[:], ssy[:, 0:1], ssy[:, 1:2], ALU.add)
                                rstd_pow(ssy1[:], ty[:], rstdy[:], 1.0 / 1024.0, 4.0 * EPS, "y1", 1)
                                for hf in range(2):
                                    hs = slice(hf * 512, (hf + 1) * 512)
                                    STT([FK[6 + hf], "y1rstd", "gpost"], ["tmp1"], tmp1[:, hs], Fb[6 + hf][:, :], rstdy[:, 0:1], gpost[:, hs], ALU.mult, ALU.mult)
                                TT("dve", ["tmp1", f"p3xt{d}"], [f"x1_{d}"], x1[d][:], tmp1[:], xt[d][:], ALU.add)
                                f = P.dma("sp", x1_scr[i], x1[d][:], R=[f"x1_{d}"], W=[("x1", i)])
                                if dbg:
                                    finals.append(f)

                            def sq_h():
                                ACT([f"x1_{d}"], ["junk3", "hss"], out=junk3[:], in_=x1[d][:], func=AF.Square, accum_out=ssh[:, 0:1])

                            def hb_step():
                                ACT([f"x1_{d}", "hrstd"], ["hb2"], out=hb2[:], in_=x1[d][:], func=AF.Identity, scale=rstdh[:, 0:1])

                            def tr_h(k0=0, k1=8):
                                for kc in range(k0, k1):
                                    TR(["hb2", "id16"], [FK[7]], Fbf[7][:, kc * 128:(kc + 1) * 128], hb2[:, kc * 128:(kc + 1) * 128], id16[:])

                            def h2_step():
                                TT("dve", [FK[7], "gffn"], [f"h2T{d}"], h2T[d][:], Fbf[7][:, 0:1024], gffn[:], ALU.mult)
                                P.dma("sp", h2T_scr[i], h2T[d][:], R=[f"h2T{d}"], W=[("h2T", i)])

                            nop = lambda: None
                            seq = lambda *fs: (lambda: [f() for f in fs])
                            L_ = lambda f, *a: (lambda: f(*a))
                            pending.extend([
                                nop, nop, tr_onsa, nop,
                                seq(lambda: CP("act", [FK[6]], ["onT"], onT[:].rearrange("p k t -> p (k t)"), Fbf[6][:, 0:512]), L_(mm_wmb, 0, 0, 1)), L_(mm_wmb, 0, 1, 2),
                                L_(mm_wmb, 0, 2, 3), L_(mm_wmb, 0, 3, 4), L_(mm_wmb, 0, 4, 5), L_(mm_wmb, 0, 5, 6), L_(mm_wmb, 0, 6, 7), L_(mm_wmb, 0, 7, 8),
                                L_(mm_wpn, 0, 0, 1), L_(mm_wpn, 0, 1, 2), seq(L_(tanh_mg, 0), L_(mm_wpn, 0, 2, 3)), L_(mm_wpn, 0, 3, 4),
                                L_(mm_wmb, 1, 0, 1), L_(mm_wmb, 1, 1, 2),
                                seq(L_(mix, 0), L_(mm_wmb, 1, 2, 3)), L_(mm_wmb, 1, 3, 4), L_(mm_wmb, 1, 4, 5), L_(mm_wmb, 1, 5, 6), L_(mm_wmb, 1, 6, 7), L_(mm_wmb, 1, 7, 8),
                                L_(mm_wpn, 1, 0, 1), L_(mm_wpn, 1, 1, 2), seq(L_(tanh_mg, 1), L_(mm_wpn, 1, 2, 3)), L_(mm_wpn, 1, 3, 4),
                                nop,
                                L_(mix, 1),
                                nop, nop,
                                L_(tr_mixed, 0, 4), L_(tr_mixed, 4, 8),
                                nop,
                                lambda: CP("act", [FK[7]], ["mixT"], mixT[:].rearrange("p k t -> p (k t)"), Fbf[7][:, 0:1024]),
                                nop,
                                L_(mm_wo, 0, 0, 1), L_(mm_wo, 0, 1, 2), L_(mm_wo, 0, 2, 3), L_(mm_wo, 0, 3, 4), L_(mm_wo, 0, 4, 5), L_(mm_wo, 0, 5, 6), L_(mm_wo, 0, 6, 7), L_(mm_wo, 0, 7, 8),
                                L_(mm_wo, 1, 0, 1), L_(mm_wo, 1, 1, 2), L_(mm_wo, 1, 2, 3), L_(mm_wo, 1, 3, 4), L_(mm_wo, 1, 4, 5), L_(mm_wo, 1, 5, 6), L_(mm_wo, 1, 6, 7), L_(mm_wo, 1, 7, 8),
                                nop,
                                sq_y,
                                nop,
                                x1_step,
                                nop, nop, nop,
                                sq_h,
                                nop,
                                lambda: rstd_pow(ssh[:], th[:], rstdh[:], 1.0 / 1024.0, EPS, "h", 1),
                                nop, nop,
                                hb_step,
                                nop,
                                L_(tr_h, 0, 4), L_(tr_h, 4, 8),
                                nop,
                                h2_step,
                            ])

                        loads3(0)
                        loads3(1)
                        for i in range(NT_OWN):
                            tile(i)
                            if i + 2 < NT_OWN:
                                loads3(i + 2)
                        while pending:
                            pending.pop(0)()
                    P.fence()

        if stop_after >= 4:
            with contextlib.ExitStack() as st:
                sb = lambda name, shape, dt: st.enter_context(nc.sbuf_tensor("sb_" + name, list(shape), dt))
                wd = sb("wd", [128, NFC, 1024], BF16)
                gp2 = sb("gp2", [128, 1024], F32)
                P.dma("act", gp2[:], I["gpost2_rep"], W=["gp2"])
                h2 = sb("h2", [128, 8, 1024], BF16)
                actT = sb("actT", [128, NFC, 1024], BF16)
                wgc = [sb(f"wgc{i}", [128, 8, 128], BF16) for i in range(3)]
                wuc = [sb(f"wuc{i}", [128, 8, 128], BF16) for i in range(3)]
                sg = [sb(f"sg{i}", [128, 512], F32) for i in range(2)]
                x1t = [sb(f"x1t{i}", [128, 1024], F32) for i in range(2)]
                ot = [sb(f"ot{i}", [128, 1024], F32) for i in range(2)]
                tmp4 = sb("tmp4", [128, 1024], F32)
                junk4 = sb("junk4", [128, 512], BF16)
                ssf = [sb(f"ssf{i}", [128, 2], F32) for i in range(2)]
                ssf1 = [sb(f"ssf1{i}", [128, 1], F32) for i in range(2)]
                tf = [sb(f"tf{i}", [128, 1], F32) for i in range(2)]
                rstdf = [sb(f"rstdf{i}", [128, 1], F32) for i in range(2)]
                it = 0
                wd_loaded = False
                def h2load(tg):
                    for j in range(8):
                        i = tg * 8 + j
                        P.dma("sp", h2[:, :, j * 128:(j + 1) * 128], h2T_scr[i].rearrange("p (k t) -> p k t", t=128), R=[("h2T", i)], W=[f"h2_{j}"])
                H2K = [f"h2_{j}" for j in range(8)]
                h2load(0)
                for tg in range(2):
                    for c in range(NFC):
                        s = it % 3
                        it += 1
                        cs = slice(c * 128, (c + 1) * 128)
                        wload(wgc[s][:].rearrange("p k f -> p (k f)"), I["wfg"][c], f"wgc{s}")
                        wload(wuc[s][:].rearrange("p k f -> p (k f)"), I["wfu"][c], f"wuc{s}")
                        if not wd_loaded and c >= 2:
                            c0 = (c - 2) * 2
                            if c0 < NFC:
                                wload(wd[:, c0:c0 + 2, :], I["wfd"][c0 * 128:(c0 + 2) * 128, :].rearrange("(c p) d -> p c d", p=128), "wd")
                            if c0 + 2 >= NFC:
                                wd_loaded = True
                        for sub in range(2):
                            ts_ = slice(sub * 512, (sub + 1) * 512)
                            bg, bu = (0, 1) if sub == 0 else (2, 3)
                            for kc in range(8):
                                MM(H2K + [f"wgc{s}"], [FK[bg]], Fb[bg][:, :], wgc[s][:, kc, :], h2[:, kc, ts_], kc == 0, kc == 7)
                            for kc in range(8):
                                MM(H2K + [f"wuc{s}"], [FK[bu]], Fb[bu][:, :], wuc[s][:, kc, :], h2[:, kc, ts_], kc == 0, kc == 7)
                            ACT([FK[bg]], [f"sg{sub}"], out=sg[sub][:], in_=Fb[bg][:, :], func=AF.Silu)
                            TT("dve", [FK[bu], f"sg{sub}"], ["actT"], actT[:, c, ts_], Fb[bu][:, :], sg[sub][:], ALU.mult)
                    if tg == 0:
                        h2load(1)
                    for j in range(8):
                        i = tg * 8 + j
                        s = i % 2
                        P.dma("sp", x1t[s][:], x1_scr[i], R=[("x1", i)], W=[f"x1t{s}"])
                        for hf in range(2):
                            hs = slice(hf * 512, (hf + 1) * 512)
                            bk = 4 + 2 * s + hf
                            for c in range(NFC):
                                MM(["actT", "wd"], [FK[bk]], Fb[bk][:, :], actT[:, c, j * 128:(j + 1) * 128], wd[:, c, hs], c == 0, c == NFC - 1)
                            ACT([FK[bk]], ["junk4", f"f{s}ss"], out=junk4[:], in_=Fb[bk][:, :], func=AF.Square, accum_out=ssf[s][:, hf:hf + 1])
                        TT("dve", [f"f{s}ss"], [f"f1{s}ss"], ssf1[s][:], ssf[s][:, 0:1], ssf[s][:, 1:2], ALU.add)
                        rstd_pow(ssf1[s][:], tf[s][:], rstdf[s][:], 1.0 / 1024.0, EPS, f"f1{s}", 1)
                        for hf in range(2):
                            hs = slice(hf * 512, (hf + 1) * 512)
                            bk = 4 + 2 * s + hf
                            STT([FK[bk], f"f1{s}rstd", "gp2"], ["tmp4"], tmp4[:, hs], Fb[bk][:, :], rstdf[s][:, 0:1], gp2[:, hs], ALU.mult, ALU.mult)
                        TT("dve", ["tmp4", f"x1t{s}"], [f"ot{s}"], ot[s][:], tmp4[:], x1t[s][:], ALU.add)
                        finals.append(P.dma("sp", out[i * 128:(i + 1) * 128, :], ot[s][:], R=[f"ot{s}"]))
        if not finals:
            finals.append(len(P.ops) - 1)
        stats = P.emit(final_wait_ops=finals)
    return nc, stats


def const_tables(half):
    T0 = half * 2048
    t = {}
    u = np.arange(4096)
    absu = u - 2048 + T0
    t["kaug"] = np.stack([u % 128, u // 128, np.ones(4096), np.ones(4096), np.where(absu >= 0, 0.0, NEGB)]).astype(np.float32)
    qa = np.zeros((2, 16, 5, 512), np.float32)
    iq = np.arange(128)
    for g in range(2):
        for i in range(16):
            for r in range(4):
                sl = 2.0 ** (-(g * 4 + r + 1))
                c = slice(r * 128, (r + 1) * 128)
                qa[g, i, 0, c] = sl
                qa[g, i, 1, c] = 128 * sl
                qa[g, i, 2, c] = -sl * iq
                qa[g, i, 3, c] = -128 * sl * (16 + i)
                qa[g, i, 4, c] = 1.0
    t["qaug"] = qa
    n = np.arange(256)
    end = 16 * n + 31
    ca = np.stack([end % 128, end // 128, np.ones(256), np.ones(256), np.where(16 * n - 2048 + T0 >= 0, 0.0, NEGB)]).astype(np.float32)
    ca[:, 255] = [0, 0, 1, 1, NEGB]
    t["caug"] = ca
    shc = np.zeros((16, 16, 256), np.float32)
    for i in range(16):
        rel = n - 8 * (16 + i)
        for uu in range(8):
            shc[i, uu, rel + 1 == uu] = 1.0
        shc[i, 8, rel >= 7] = 1.0
    t["shc"] = shc
    tc = np.zeros((16, 512), np.float32)
    for uu in range(8):
        for r in range(4):
            tc[uu, r * 128:(r + 1) * 128] = np.where(iq >= 15 + 16 * uu, 0.0, NEGB)
    tc[8, :] = NEGB
    t["tc"] = tc
    j = np.arange(64)
    ovm = np.clip(np.minimum(16 * n[:, None] + 32, 64 * j[None, :] + 64) - np.maximum(16 * n[:, None], 64 * j[None, :]), 0, None) / 32.0
    ovm[255, :] = 0.0
    t["ov"] = np.ascontiguousarray(ovm.reshape(2, 128, 64).transpose(1, 0, 2)).astype(np.float32)
    selA = np.zeros((16, 128, 64), np.float32)
    selB = np.zeros((16, 128, 64), np.float32)
    for i in range(16):
        tq = 128 * i + iq + T0
        cur = tq // 64
        jb = j[None, :] - 32 + T0 // 64
        forced = (jb >= 0) & ((jb == 0) | (jb == cur[:, None]) | (jb == cur[:, None] - 1))
        bad = (jb < 0) | (jb > cur[:, None])
        selA[i] = np.where(forced, 1e30, -1e30)
        selB[i] = np.where(bad, -1e30, 1e30)
    t["selA"], t["selB"] = selA, selB
    t["eall"] = (u[None, :] // 64 == j[:, None]).astype(np.float32)
    ekt = np.zeros((59, 4096), np.float32)
    jb_ = u // 64
    row = np.where(u // 128 <= 28, jb_, jb_ - 5)
    ekt[row, u] = 1.0
    t["ekt"] = ekt
    ik = np.arange(128)
    trib = np.where(ik[:, None] <= iq[None, :], 0.0, NEGB)
    trilo = np.where(ik[:, None] > iq[None, :], 0.0, NEGB)
    t["trib"] = np.tile(trib, (1, 4)).astype(np.float32)
    t["trilo"] = np.tile(trilo, (1, 4)).astype(np.float32)
    t["ident"] = np.eye(128, dtype=np.float32)
    t["tril"] = (ik[:, None] <= iq[None, :]).astype(np.float32)
    t["gmask"] = np.tile((ik[:, None] <= iq[None, :]).astype(np.float32), (1, 4))
    return t


def featmajor_rep(g):
    return np.ascontiguousarray(np.repeat(g.reshape(8, 128).T[:, :, None], 128, axis=2).reshape(128, 1024)).astype(np.float32)


def chunk_major(w):
    return np.ascontiguousarray(np.asarray(w).reshape(8, 128, NFC, 128).transpose(2, 1, 0, 3).reshape(NFC, 128, 1024))


def make_in_maps(inp):
    f = lambda a: np.ascontiguousarray(np.asarray(a, dtype=np.float32))
    x = f(inp["x"])
    shared = dict(
        w_in=f(inp["w_in"][0]),
        walpha=f(np.concatenate([inp["gla_w_alpha2"][0], inp["gla_b_alpha"][0][None, :]], axis=0)),
        ngrep=f(np.tile(np.tile(inp["gla_norm_g"][0], 4)[None, :], (128, 1))),
        gpre_b=featmajor_rep(f(inp["norm_mix_pre"][0])),
        gffn_b=featmajor_rep(f(inp["norm_ffn_pre"][0])),
        gpost_rep=f(np.tile(inp["norm_mix_post"][0][None, :], (128, 1))),
        gpost2_rep=f(np.tile(inp["norm_ffn_post"][0][None, :], (128, 1))),
        w1k=f(inp["nsa_cmp_w1_k"][0].reshape(32, 64, 64).transpose(1, 0, 2).reshape(64, 2048)), w2k=f(inp["nsa_cmp_w2_k"][0]),
        w1v=f(inp["nsa_cmp_w1_v"][0].reshape(32, 64, 64).transpose(1, 0, 2).reshape(64, 2048)), w2v=f(inp["nsa_cmp_w2_v"][0]),
        pek=f(np.repeat(np.tile(inp["nsa_cmp_pe_k"][0].T, (2, 1))[:, :, None], 2, axis=2)),
        pev=f(np.repeat(np.tile(inp["nsa_cmp_pe_v"][0].T, (2, 1))[:, :, None], 2, axis=2)),
        wpg=f(inp["w_proj_gla"][0]), wpn=f(inp["w_proj_nsa"][0]), wo=f(inp["w_out"][0]),
        wfg=f(chunk_major(inp["w_ffn_gate"][0])), wfu=f(chunk_major(inp["w_ffn_up"][0])), wfd=f(inp["w_ffn_down"][0]),
    )
    tabs = [const_tables(0), const_tables(1)]
    maps = []
    for c in range(8):
        b, half = c // 2, c % 2
        xc = np.zeros((4096, 1024), np.float32)
        xc[2048:] = x[b, half * 2048:(half + 1) * 2048]
        if half == 1:
            xc[:2048] = x[b, 0:2048]
        m = dict(shared)
        m.update(tabs[half])
        m["xc"] = xc
        for k, s in IN_SPECS.items():
            assert m[k].shape == tuple(s), (k, m[k].shape, s)
        maps.append({k: m[k] for k in IN_SPECS})
    return maps


_CACHE = {}


def kernel(**inputs):
    if "nc" not in _CACHE:
        _CACHE["nc"] = build()[0]
    nc = _CACHE["nc"]
    maps = make_in_maps(inputs)
    res = run_bass_kernel_spmd(nc, maps, core_ids=list(range(8)))
    outp = np.zeros((4, 4096, 1024), np.float32)
    for c in range(8):
        b, half = c // 2, c % 2
        outp[b, half * 2048:(half + 1) * 2048] = np.asarray(res.results[c]["out"], dtype=np.float32)
    return outp
```

```python
import contextlib
import numpy as np
import concourse.bass as bass
import concourse.mybir as mybir
from concourse.bass_utils import run_bass_kernel_spmd

F32, BF16 = mybir.dt.float32, mybir.dt.bfloat16
AF = mybir.ActivationFunctionType
ALU = mybir.AluOpType
NEGB = -30000.0
ENG = ("pe", "act", "dve", "pool", "sp")
NT_CTX, NT_OWN = 32, 16
EPS = 1e-6
D_FF = 2816
NFC = D_FF // 128


class Prog:
    def __init__(self, nc, n_dma_sems=16):
        self.nc = nc
        self.ops = []
        self.lastw = {}
        self.readers = {}
        self.deps = []
        self.n_dma_sems = n_dma_sems
        self.fence_deps = set()
        self.fence_pending = set()
        self.last_eng = {}
        self.last_dma = {}
        self.dma_rr = {e: 0 for e in ENG}
        self.slot = []
        self.excl = set()
        self.alias = {}

    def op(self, eng, fn, R=(), W=(), dma=False):
        i = len(self.ops)
        d = set()
        if self.alias:
            R = [p for k in R for p in self.alias.get(k, (k,))]
            W = [p for k in W for p in self.alias.get(k, (k,))]
        for k in R:
            if k in self.lastw:
                d.add(self.lastw[k])
            if k in self.excl:
                for r in self.readers.get(k, ()):
                    if self.ops[r][0] != eng:
                        d.add(r)
        for k in W:
            if k in self.lastw:
                d.add(self.lastw[k])
            for r in self.readers.get(k, ()):
                d.add(r)
        if eng in self.fence_pending:
            d |= self.fence_deps
            self.fence_pending.discard(eng)
        d.discard(i)
        for k in R:
            self.readers.setdefault(k, []).append(i)
        for k in W:
            self.lastw[k] = i
            self.readers[k] = []
        self.ops.append((eng, fn, dma))
        self.deps.append(d)
        if dma:
            s = (eng, self.dma_rr[eng] % (8 if eng == "pool" else self.n_dma_sems))
            self.dma_rr[eng] += 1
            self.slot.append(s)
            if s in self.last_dma:
                d.add(self.last_dma[s])
            self.last_dma[s] = i
        else:
            self.slot.append(None)
            self.last_eng[eng] = i
        return i

    def dma(self, q, out, in_, R=(), W=()):
        return self.op(q, lambda e: e.dma_start(out=out, in_=in_), R, W, dma=True)

    def fence(self):
        self.fence_deps = set(self.last_eng.values()) | set(self.last_dma.values())
        self.fence_pending = set(ENG)

    def emit(self, final_wait_ops=()):
        nc = self.nc
        ops, deps = self.ops, self.deps
        n = len(ops)
        for i in range(n):
            best = {}
            for j in deps[i]:
                k = self.slot[j] if ops[j][2] else ops[j][0]
                if k not in best or j > best[k]:
                    best[k] = j
            deps[i] = set(best.values())
        need = [False] * n
        for i in range(n):
            ei, _, di = ops[i]
            for j in deps[i]:
                ej, _, dj = ops[j]
                if dj or di or not (ei == "pe" and ej == "pe"):
                    need[j] = True
        for j in final_wait_ops:
            need[j] = True
        cnt = {e: 0 for e in ENG}
        dcnt = {}
        tok = [None] * n
        for i in range(n):
            e, _, d = ops[i]
            if d:
                s = self.slot[i]
                dcnt[s] = dcnt.get(s, 0) + 16
                tok[i] = (("dma",) + s, dcnt[s])
            elif need[i]:
                cnt[e] += 1
                tok[i] = (("eng", e), cnt[e])
        semkeys = sorted({t[0] for t in tok if t is not None})
        with contextlib.ExitStack() as st:
            sems = {k: st.enter_context(nc.semaphore("s_" + "_".join(map(str, k)))) for k in semkeys}
            block = st.enter_context(nc.Block())
            per = {e: [] for e in ENG}
            for i in range(n):
                per[ops[i][0]].append(i)
            final = {}
            for j in final_wait_ops:
                final.setdefault(ops[j][0], []).append(j)

            def body(e, handle):
                waited = {}
                for i in per[e]:
                    _, fn, d = ops[i]
                    for j in sorted(deps[i]):
                        t = tok[j]
                        if t is None:
                            continue
                        if e == "pe" and ops[j][0] == "pe" and not d and not ops[j][2]:
                            continue
                        if waited.get(t[0], 0) >= t[1]:
                            continue
                        handle.wait_ge(sems[t[0]], t[1])
                        waited[t[0]] = t[1]
                    ins = fn(handle)
                    if tok[i] is not None:
                        ins.then_inc(sems[tok[i][0]], 16 if d else 1)
                for j in final.get(e, ()):
                    t = tok[j]
                    if waited.get(t[0], 0) < t[1]:
                        handle.wait_ge(sems[t[0]], t[1])
                        waited[t[0]] = t[1]

            reg = {"pe": block.tensor, "act": block.scalar, "dve": block.vector,
                   "pool": block.gpsimd, "sp": block.sync}
            for e in ENG:
                if per[e] or final.get(e):
                    reg[e](lambda h, e=e: body(e, h))
        return {e: len(per[e]) for e in ENG}, cnt


IN_SPECS = dict(
    xc=(4096, 1024), w_in=(1024, 6440), walpha=(17, 512), ngrep=(128, 1024),
    gpre_b=(128, 1024), gffn_b=(128, 1024), gpost_rep=(128, 1024), gpost2_rep=(128, 1024),
    w1k=(64, 2048), w2k=(64, 64), pek=(128, 32, 2), w1v=(64, 2048), w2v=(64, 64), pev=(128, 32, 2),
    wpg=(1024, 1024), wpn=(512, 1024), wo=(1024, 1024),
    wfg=(NFC, 128, 1024), wfu=(NFC, 128, 1024), wfd=(D_FF, 1024),
    kaug=(5, 4096), qaug=(2, 16, 5, 512), caug=(5, 256), shc=(16, 16, 256), tc=(16, 512),
    ov=(128, 2, 64), selA=(16, 128, 64), selB=(16, 128, 64), eall=(64, 4096),
    ekt=(59, 4096), trib=(128, 512), trilo=(128, 512), ident=(128, 128), tril=(128, 128), gmask=(128, 512),
)


def build(stop_after=4, dbg=False):
    nc = bass.Bass("TRN2", target_bir_lowering=False)
    I = {k: nc.dram_tensor(k, list(s), F32, kind="ExternalInput").ap() for k, s in IN_SPECS.items()}
    out = nc.dram_tensor("out", [2048, 1024], F32, kind="ExternalOutput").ap()
    skind = "ExternalOutput" if dbg else "Internal"
    hT_scr = nc.dram_tensor("hT_scr", [NT_CTX, 128, 1024], BF16, kind=skind).ap()
    E_scr = nc.dram_tensor("E_scr", [NT_CTX, 128, 512], F32, kind="Internal").ap()
    mixA_scr = nc.dram_tensor("mixA_scr", [NT_OWN, 128, 1024], F32, kind=skind).ap()
    x1_scr = nc.dram_tensor("x1_scr", [NT_OWN, 128, 1024], F32, kind=skind).ap()
    h2T_scr = nc.dram_tensor("h2T_scr", [NT_OWN, 128, 1024], BF16, kind=skind).ap()
    dbgt = {}
    if dbg:
        dbgt["og"] = nc.dram_tensor("og_dbg", [NT_OWN, 128, 1024], BF16, kind="ExternalOutput").ap()
        dbgt["onsa"] = nc.dram_tensor("onsa_dbg", [NT_OWN, 128, 512], BF16, kind="ExternalOutput").ap()

    P = Prog(nc)
    finals = []

    def ACT(R, W, **kw):
        return P.op("act", lambda e: e.activation(**kw), R, W)

    def MM(R, W, out, lhsT, rhs, start, stop):
        return P.op("pe", lambda e: e.matmul(out, lhsT=lhsT, rhs=rhs, start=start, stop=stop,
                                             skip_group_check=True), R, W)

    def TR(R, W, out, in_, ident):
        return P.op("pe", lambda e: e.transpose(out, in_, ident), R, W)

    def TT(eng, R, W, out, in0, in1, op):
        return P.op(eng, lambda e: e.tensor_tensor(out=out, in0=in0, in1=in1, op=op), R, W)

    def TS(eng, R, W, out, in0, s1, s2, op0, op1=None):
        if op1 is None:
            return P.op(eng, lambda e: e.tensor_scalar(out=out, in0=in0, scalar1=s1, scalar2=None, op0=op0), R, W)
        return P.op(eng, lambda e: e.tensor_scalar(out=out, in0=in0, scalar1=s1, scalar2=s2, op0=op0, op1=op1), R, W)

    def STT(R, W, out, in0, scalar, in1, op0, op1):
        return P.op("dve", lambda e: e.scalar_tensor_tensor(out=out, in0=in0, scalar=scalar, in1=in1,
                                                            op0=op0, op1=op1), R, W)

    def CP(eng, R, W, out, in_):
        if eng == "act":
            return P.op("act", lambda e: e.copy(out=out, in_=in_), R, W)
        return P.op(eng, lambda e: e.tensor_copy(out=out, in_=in_), R, W)

    def RECIP(R, W, out, in_):
        return P.op("dve", lambda e: e.reciprocal(out=out, in_=in_), R, W)

    def MEMSET(eng, W, ap, val):
        return P.op(eng, lambda e: e.memset(ap, val), (), W)

    def wload(dst, src, key):
        n = dst.shape[-1]
        if n > 2048:
            assert len(dst.shape) == 2
            r = None
            for c0 in range(0, n, 2048):
                r = P.dma("pool", dst[:, c0:c0 + 2048], src[:, c0:c0 + 2048], W=[key + f"_{c0}"])
            P.alias[key] = [key + f"_{c0}" for c0 in range(0, n, 2048)]
            return r
        return P.dma("pool", dst, src, W=[key])

    with contextlib.ExitStack() as gst:
        def gsb(name, shape, dt):
            return gst.enter_context(nc.sbuf_tensor("sb_" + name, list(shape), dt))
        Fb = [gst.enter_context(nc.psum_tensor(f"F{i}", [128, 512], F32)) for i in range(8)]
        FK = [f"F{i}" for i in range(8)]
        P.excl = set(FK)
        Fbf = [f[:].bitcast(BF16) for f in Fb]
        id16 = gsb("id16", [128, 128], BF16)
        wload(id16[:], I["ident"], "id16")
        wl_dummy = gsb("wl_dummy", [128, 1], F32)
        nhalf = gsb("nhalf", [128, 4], F32)
        MEMSET("pool", ["nhalf"], nhalf[:], -0.5)

        def rstd_pow(ss, tmp, rstd, mul, eps, key, ncol):
            TS("dve", [key + "ss"], [key + "t"], tmp, ss, mul, eps, ALU.mult, ALU.add)
            return P.op("pool", lambda e: e.tensor_tensor(out=rstd, in0=tmp, in1=nhalf[:, 0:ncol], op=ALU.pow),
                        [key + "t", "nhalf"], [key + "rstd"])

        wkv = gsb("wkv", [128, 8, 768], BF16)
        stW1 = contextlib.ExitStack()
        sbw = lambda name, shape, dt: stW1.enter_context(nc.sbuf_tensor("sb_" + name, list(shape), dt))
        wk = sbw("wk", [128, 8, 512], BF16)
        wv = sbw("wv", [128, 8, 1024], BF16)
        wq = sbw("wq", [128, 8, 512], BF16)
        wr = sbw("wr", [128, 8, 1024], BF16)
        wv4g = lambda c0, c1: I["w_in"][:, c0:c1].rearrange("(kc p) c -> p kc c", p=128)

        with contextlib.ExitStack() as st:
            sb = lambda name, shape, dt: st.enter_context(nc.sbuf_tensor("sb_" + name, list(shape), dt))
            NS = 3
            NX = 5
            xt = [sb(f"p0xt{i}", [128, 1024], F32) for i in range(NX)]
            hb = [sb(f"p0hb{i}", [128, 1024], BF16) for i in range(NS)]
            hTs = [sb(f"p0hT{i}", [128, 8, 128], BF16) for i in range(NS)]
            ss0 = [sb(f"p0ss{i}", [128, 1], F32) for i in range(NS)]
            tm0 = [sb(f"p0tm{i}", [128, 1], F32) for i in range(NS)]
            rs0 = [sb(f"p0rs{i}", [128, 1], F32) for i in range(NS)]
            st6 = [sb(f"p0st6{i}", [128, 12], F32) for i in range(NS)]
            mv0 = [sb(f"p0mv{i}", [128, 2], F32) for i in range(NS)]
            aTs = [sb(f"p0aT{i}", [17, 128], BF16) for i in range(2)]
            ezs = [sb(f"p0ez{i}", [128, 512], F32) for i in range(2)]
            L32 = [sb(f"p0L{i}", [128, 512], F32) for i in range(2)]
            Eb = [sb(f"p0E{i}", [128, 512], F32) for i in range(2)]
            junk0 = sb("p0junk", [128, 1024], BF16)
            gb = sb("p0gb", [128, 1024], F32)
            wa = sb("p0wa", [128, 8, 16], BF16)
            walpha = sb("p0walpha", [17, 512], BF16)
            tril = sb("p0tril", [128, 128], F32)
            P.dma("act", gb[:], I["gpre_b"], W=["gb0"])
            P.dma("act", tril[:], I["tril"], W=["tril"])
            wload(wa[:], I["w_in"][:, 3072:3088].rearrange("(kc p) c -> p kc c", p=128), "wa")
            wload(walpha[:], I["walpha"], "walpha")
            for i in range(2):
                MEMSET("dve", [f"aT{i}"], aTs[i][:], 1.0)
            wload(wk[:], wv4g(512, 1024), "wk")
            wload(wv[:], wv4g(1024, 2048), "wv")
            wload(wq[:], wv4g(0, 512), "wq")
            wload(wr[:], wv4g(2048, 3072), "wr")
            H = lambda hd: slice(hd * 128, (hd + 1) * 128)

            def xload(n):
                P.dma("sp", xt[n % NX][:], I["xc"][n * 128:(n + 1) * 128, :], W=[f"xt{n % NX}"])

            def s1a(n):
                s = n % NS
                x_ = n % NX
                if n + 2 < NT_CTX:
                    xload(n + 2)
                for c in range(2):
                    P.op("dve", lambda e, o_=st6[s][:, c * 6:(c + 1) * 6], i_=xt[x_][:, c * 512:(c + 1) * 512]: e.bn_stats(out=o_, in_=i_), [f"xt{x_}"], [f"st6{s}"])
                P.op("dve", lambda e, o_=mv0[s][:], i_=st6[s][:]: e.bn_aggr(out=o_, in_=i_), [f"st6{s}"], [f"mv{s}"])
                STT([f"mv{s}"], [f"n0{s}ss"], ss0[s][:], mv0[s][:, 0:1], mv0[s][:, 0:1], mv0[s][:, 1:2], ALU.mult, ALU.add)
                rstd_pow(ss0[s][:], tm0[s][:], rs0[s][:], 1.0, EPS, f"n0{s}", 1)

            def s1b(n):
                s = n % NS
                a2 = n % 2
                ACT([f"xt{n % NX}", f"n0{s}rstd"], [f"hb{s}"], out=hb[s][:], in_=xt[n % NX][:], func=AF.Identity, scale=rs0[s][:, 0:1])
                bank = 6 + a2
                for kc in range(8):
                    TR([f"hb{s}", "id16"], [FK[bank]], Fbf[bank][:, kc * 128:(kc + 1) * 128], hb[s][:, kc * 128:(kc + 1) * 128], id16[:])
                TT("dve", [FK[bank], "gb0"], [f"hTs{s}"], hTs[s][:].rearrange("p k t -> p (k t)"), Fbf[bank][:, 0:1024], gb[:], ALU.mult)
                P.dma("sp", hT_scr[n], hTs[s][:].rearrange("p k t -> p (k t)"), R=[f"hTs{s}"], W=[("hT", n)])
                for kc in range(8):
                    MM([f"hTs{s}", "wa"], [FK[a2]], Fb[a2][0:16, 0:128], wa[:, kc, :], hTs[s][:, kc, :], kc == 0, kc == 7)
                CP("dve", [FK[a2]], [f"aT{a2}"], aTs[a2][0:16, :], Fb[a2][0:16, 0:128])
                MM([f"aT{a2}", "walpha"], [FK[2 + a2]], Fb[2 + a2][:, :], aTs[a2][:, :], walpha[:, :], True, True)

            def s2(n):
                a2 = n % 2
                ACT([FK[2 + a2]], [f"ez{a2}"], out=ezs[a2][:], in_=Fb[2 + a2][:, :], func=AF.Exp, scale=-1.0)
                ACT([f"ez{a2}"], [f"L{a2}"], out=L32[a2][:], in_=ezs[a2][:], func=AF.Ln, bias=1.0)
                for hd in range(4):
                    MM([f"L{a2}", "tril"], [FK[4 + a2]], Fb[4 + a2][:, H(hd)], L32[a2][:, H(hd)], tril[:, :], True, True)

            def s3(n):
                a2 = n % 2
                CP("act", [FK[4 + a2]], [f"E{a2}"], Eb[a2][:], Fb[4 + a2][:, :])
                P.dma("sp", E_scr[n], Eb[a2][:], R=[f"E{a2}"], W=[("E", n)])

            xload(0)
            xload(1)
            for k in range(NT_CTX + 3):
                if k < NT_CTX:
                    s1a(k)
                if 0 <= k - 1 < NT_CTX:
                    s1b(k - 1)
                if 0 <= k - 2 < NT_CTX:
                    s2(k - 2)
                if 0 <= k - 3 < NT_CTX:
                    s3(k - 3)
        P.fence()

        if stop_after >= 1:
            with contextlib.ExitStack() as st:
                sb = lambda name, shape, dt: st.enter_context(nc.sbuf_tensor("sb_" + name, list(shape), dt))
                wma = sb("wma", [128, 8, 1024], BF16)
                wpg = sb("wpgs", [128, 8, 1024], BF16)
                win = I["w_in"]
                wv4 = lambda c0, c1: win[:, c0:c1].rearrange("(kc p) c -> p kc c", p=128)
                wload(wma[:], wv4(4392, 5416), "wma")
                wload(wpg[:], I["wpg"].rearrange("(kc p) c -> p kc c", p=128), "wpg")
                wload(wkv[:], I["w_in"][:, 3600:4368].rearrange("(kc p) c -> p kc c", p=128), "wkv")
                gmask = sb("gmask", [128, 512], F32)
                P.dma("act", gmask[:], I["gmask"], W=["gmask"])
                ngrep = sb("ngrep", [128, 1024], F32)
                P.dma("act", ngrep[:], I["ngrep"], W=["ngrep"])
                NS = 3
                hTt = [sb(f"p1hT{i}", [128, 8, 128], BF16) for i in range(NS)]
                Et = [sb(f"p1E{i}", [128, 1024], F32) for i in range(NS)]
                BTt = [sb(f"p1BT{i}", [128, 512], F32) for i in range(NS)]
                keT = [sb(f"keT{i}", [128, 512], BF16) for i in range(2)]
                qeT = [sb(f"qeT{i}", [128, 512], BF16) for i in range(2)]
                ketm = [sb(f"ketm{i}", [128, 512], BF16) for i in range(2)]
                attm = [sb(f"attm{i}", [128, 512], BF16) for i in range(2)]
                v16 = [sb(f"v16{i}", [128, 1024], BF16) for i in range(2)]
                S32 = sb("S32", [128, 1024], F32)
                S16 = sb("S16", [128, 1024], BF16)
                MEMSET("dve", ["S32"], S32[:], 0.0)
                MEMSET("pool", ["S16"], S16[:], 0.0)
                tr = sb("tr", [128, 1024], F32)
                sr = sb("sr", [128, 1024], F32)
                ngsr = [sb(f"ngsr{i}", [128, 1024], F32) for i in range(2)]
                ss4 = [sb(f"ss4{i}", [128, 4], F32) for i in range(2)]
                t4 = [sb(f"t4{i}", [128, 4], F32) for i in range(2)]
                rstd4 = [sb(f"rstd4{i}", [128, 4], F32) for i in range(2)]
                junk1 = sb("junk1", [128, 256], BF16)
                og = [sb(f"og{i}", [128, 1024], BF16) for i in range(2)]
                ogT = sb("ogT", [128, 8, 128], BF16)
                tga = sb("tga", [128, 1024], F32)
                mixA = [sb(f"mixA{i}", [128, 1024], F32) for i in range(2)]
                V = lambda hd: slice(hd * 256, (hd + 1) * 256)
                O0 = NT_CTX - NT_OWN

                def loads(n):
                    s = n % NS
                    P.dma("sp", hTt[s][:], hT_scr[n].rearrange("p (k t) -> p k t", t=128), R=[("hT", n)], W=[f"p1hT{s}"])
                    P.dma("sp", BTt[s][:], E_scr[n], R=[("E", n)], W=[f"p1BT{s}"])

                def y1(n):
                    s = n % NS
                    d = n % 2
                    hk = f"p1hT{s}"
                    ht = hTt[s]
                    for kc in range(8):
                        TR([f"og{d}", "id16"], [FK[5]], Fbf[5][:, kc * 128:(kc + 1) * 128], og[d][:, kc * 128:(kc + 1) * 128], id16[:])
                    for hf in range(2):
                        for kc in range(8):
                            MM([hk, "wma"], [FK[6 + hf]], Fb[6 + hf][:, :], ht[:, kc, :], wma[:, kc, hf * 512:(hf + 1) * 512], kc == 0, kc == 7)
                    CP("act", [FK[5]], ["ogT"], ogT[:].rearrange("p k t -> p (k t)"), Fbf[5][:, 0:1024])
                    for hf in range(2):
                        ACT([FK[6 + hf]], ["tga"], out=tga[:, hf * 512:(hf + 1) * 512], in_=Fb[6 + hf][:, :], func=AF.Tanh, scale=0.5)

                def y2(n):
                    d = n % 2
                    i_own = n - O0
                    for hf in range(2):
                        hs = slice(hf * 512, (hf + 1) * 512)
                        for kc in range(8):
                            MM(["ogT", "wpg"], [FK[6 + hf]], Fb[6 + hf][:, :], ogT[:, kc, :], wpg[:, kc, hs], kc == 0, kc == 7)
                        STT(["tga", FK[6 + hf]], [f"mixA{d}"], mixA[d][:, hs], tga[:, hs], 1.0, Fb[6 + hf][:, :], ALU.add, ALU.mult)
                    f = P.dma("sp", mixA_scr[i_own], mixA[d][:], R=[f"mixA{d}"], W=[("mixA", i_own)])
                    if dbg:
                        finals.append(f)

                def stage(n):
                    s = n % NS
                    d = n % 2
                    own = n >= O0
                    pown = n - 1 >= O0
                    hk, ek = f"p1hT{s}", f"p1E{s}"
                    ht = hTt[s]
                    E1, E2 = Et[s][:, 0:512], Et[s][:, 512:1024]
                    ACT([f"p1BT{s}"], [ek], out=E1, in_=BTt[s][:], func=AF.Exp, scale=-1.0 / 16.0)
                    ACT([f"p1BT{s}"], [ek], out=E2, in_=BTt[s][:], func=AF.Exp, scale=1.0 / 16.0)
                    oc_ = lambda hd: slice((hd % 2) * 256, (hd % 2) * 256 + 256)
                    for hd in range(4):
                        for kc in range(8):
                            MM([hk, "wk"], [FK[0]], Fb[0][:, H(hd)], wk[:, kc, H(hd)], ht[:, kc, :], kc == 0, kc == 7)
                    if own:
                        for hd in range(4):
                            for kc in range(8):
                                MM([hk, "wq"], [FK[1]], Fb[1][:, H(hd)], wq[:, kc, H(hd)], ht[:, kc, :], kc == 0, kc == 7)
                        for hf in range(2):
                            for kc in range(8):
                                MM([hk, "wr"], [FK[6 + hf]], Fb[6 + hf][:, :], ht[:, kc, :], wr[:, kc, hf * 512:(hf + 1) * 512], kc == 0, kc == 7)
                    for hf in range(2):
                        for kc in range(8):
                            MM([hk, "wv"], [FK[2 + hf]], Fb[2 + hf][:, :], ht[:, kc, :], wv[:, kc, hf * 512:(hf + 1) * 512], kc == 0, kc == 7)
                    TT("dve", [FK[0], ek], [f"keT{d}"], keT[d][:], Fb[0][:, :], E2, ALU.mult)
                    if own:
                        STT([FK[1], ek], [f"qeT{d}"], qeT[d][:], Fb[1][:, :], 128.0 ** -0.5, E1, ALU.mult, ALU.mult)
                        for hf in range(2):
                            ACT([FK[6 + hf]], ["tr"], out=tr[:, hf * 512:(hf + 1) * 512], in_=Fb[6 + hf][:, :], func=AF.Tanh, scale=0.5)
                    for hf in range(2):
                        CP("act", [FK[2 + hf]], [f"v16{d}"], v16[d][:, hf * 512:(hf + 1) * 512], Fb[2 + hf][:, :])
                    if own:
                        for hf in range(2):
                            hs = slice(hf * 512, (hf + 1) * 512)
                            STT(["tr", FK[6 + hf]], ["sr"], sr[:, hs], tr[:, hs], 1.0, Fb[6 + hf][:, :], ALU.add, ALU.mult)
                        TT("pool", ["sr", "ngrep"], [f"ngsr{d}"], ngsr[d][:], sr[:], ngrep[:], ALU.mult)
                    for hd in range(4):
                        TR([f"keT{d}", "id16"], [FK[5]], Fbf[5][:, H(hd)], keT[d][:, H(hd)], id16[:])
                    if own:
                        for hd in range(4):
                            MM([f"keT{d}", f"qeT{d}"], [FK[0]], Fb[0][:, H(hd)], keT[d][:, H(hd)], qeT[d][:, H(hd)], True, True)
                    CP("act", [FK[5]], [f"ketm{d}"], ketm[d][:], Fbf[5][:, 0:512])
                    if own:
                        TT("dve", [FK[0], "gmask"], [f"attm{d}"], attm[d][:], Fb[0][:, :], gmask[:], ALU.mult)
                    if pown:
                        y1(n - 1)
                    for hd in range(4):
                        bk = (1, 4)[hd // 2]
                        MM([f"ketm{d}", f"v16{d}"], [FK[bk]], Fb[bk][:, oc_(hd)], ketm[d][:, H(hd)], v16[d][:, V(hd)], True, True)
                    if own:
                        for hd in range(4):
                            bk = 2 + hd // 2
                            MM([f"attm{d}", f"v16{d}"], [FK[bk]], Fb[bk][:, oc_(hd)], attm[d][:, H(hd)], v16[d][:, V(hd)], True, False)
                            MM([f"qeT{d}", "S16"], [FK[bk]], Fb[bk][:, oc_(hd)], qeT[d][:, H(hd)], S16[:, V(hd)], False, True)
                    pass
                    for hd in range(4):
                        bk = (1, 4)[hd // 2]
                        dec = Et[s][:, hd * 128 + 127:hd * 128 + 128]
                        TS("dve", ["S32", ek], ["S32"], S32[:, V(hd)], S32[:, V(hd)], dec, None, ALU.mult)
                        STT([FK[bk], ek, "S32"], ["S32"], S32[:, V(hd)], Fb[bk][:, oc_(hd)], dec, S32[:, V(hd)], ALU.mult, ALU.add)
                    if own:
                        for hd in range(4):
                            bk = 2 + hd // 2
                            ACT([FK[bk]], ["junk1", f"g4{d}ss"], out=junk1[:], in_=Fb[bk][:, oc_(hd)], func=AF.Square, accum_out=ss4[d][:, hd:hd + 1])
                        rstd_pow(ss4[d][:], t4[d][:], rstd4[d][:], 4.0 / 256.0, 4.0 * EPS, f"g4{d}", 4)
                        for hd in range(4):
                            bk = 2 + hd // 2
                            STT([FK[bk], f"g4{d}rstd", f"ngsr{d}"], [f"og{d}"], og[d][:, V(hd)], Fb[bk][:, oc_(hd)], rstd4[d][:, hd:hd + 1], ngsr[d][:, V(hd)], ALU.mult, ALU.mult)
                        if dbg:
                            finals.append(P.dma("sp", dbgt["og"][n - O0], og[d][:], R=[f"og{d}"]))
                    CP("act", ["S32"], ["S16"], S16[:], S32[:])
                    if pown:
                        y2(n - 1)

                loads(0)
                loads(1)
                for n in range(NT_CTX):
                    stage(n)
                    if n + 2 < NT_CTX:
                        loads(n + 2)
                y1(NT_CTX - 1)
                y2(NT_CTX - 1)
            P.fence()
        stW1.close()

        if stop_after >= 2:
            with contextlib.ExitStack() as st23:
                sb23 = lambda name, shape, dt: st23.enter_context(nc.sbuf_tensor("sb_" + name, list(shape), dt))
                KT = sb23("KT", [128, 4, 4096], BF16)
                VA = sb23("VA", [128, NT_CTX, 4, 65], BF16)
                Kc = sb23("Kc", [128, 2, 256], BF16)
                Vc = sb23("Vc", [128, 2, 2, 128], BF16)
                MEMSET("dve", ["VA"], VA[:], 1.0)
                MEMSET("pool", ["Vc"] + [f"Vco{g}{nt}" for g in range(2) for nt in range(2)], Vc[:], 1.0)
                KTK = ["KT"] + [f"KT{c}{v}" for c in "eaw" for v in range(2)]
                KCK = ["Kc", "Kca0", "Kca1"]
                VCK = ["Vc"] + [f"Vco{g}{nt}" for g in range(2) for nt in range(2)]
                wnq = sb23("wnq", [128, 8, 512], BF16)
                wg = sb23("wg", [128, 8, 24], BF16)
                wmb = sb23("wmb", [128, 8, 1024], BF16)
                wpn = sb23("wpns", [128, 4, 1024], BF16)
                wo = sb23("wos", [128, 8, 1024], BF16)
                wv4 = lambda c0, c1: I["w_in"][:, c0:c1].rearrange("(kc p) c -> p kc c", p=128)
                trib = sb23("trib", [128, 512], BF16)
                trilo = sb23("trilo", [128, 512], BF16)
                tcs = sb23("tcs", [16, 512], BF16)
                gpost = sb23("gpost", [128, 1024], F32)
                gffn = sb23("gffn", [128, 1024], F32)
                with contextlib.ExitStack() as st:
                    sb = lambda name, shape, dt: st.enter_context(nc.sbuf_tensor("sb_" + name, list(shape), dt))
                    kcT = sb("kcT", [128, 4096], BF16)
                    vcT = sb("vcT", [128, 4096], BF16)
                    w1 = {"k": sb("w1k", [128, 32, 64], BF16), "v": sb("w1v", [128, 32, 64], BF16)}
                    w2 = {"k": sb("w2k", [64, 64], BF16), "v": sb("w2v", [64, 64], BF16)}
                    pe = {"k": sb("pek", [128, 32, 2], BF16), "v": sb("pev", [128, 32, 2], BF16)}
                    for kv in "kv":
                        for hh in range(2):
                            wload(w1[kv][hh * 64:(hh + 1) * 64, :, :].rearrange("p l j -> p (l j)"), I["w1" + kv], "w1" + kv + str(hh))
                        wload(w2[kv][:], I["w2" + kv], "w2" + kv)
                        wload(pe[kv][:], I["pe" + kv], "pe" + kv)
                    for v in range(2):
                        wload(KT[64:123, v, :], I["ekt"], f"KTe{v}")
                        wload(KT[123:128, v, :], I["kaug"], f"KTa{v}")
                        wload(KT[64:69, 2 + v, :], I["kaug"], f"KTw{v}")
                    for g in range(2):
                        wload(Kc[64:69, g, :], I["caug"], f"Kca{g}")
                        for nt in range(2):
                            wload(Vc[:, g, nt, 64:128], I["ov"][:, nt, :], f"Vco{g}{nt}")
                    wload(wnq[:], wv4(3088, 3600), "wnq")
                    wload(wg[:], wv4(4368, 4392), "wg")
                    wload(wmb[:], wv4(5416, 6440), "wmb")
                    wload(wpn[:], I["wpn"].rearrange("(kc p) c -> p kc c", p=128), "wpn")
                    wload(wo[:], I["wo"].rearrange("(kc p) c -> p kc c", p=128), "wo")
                    wload(trib[:], I["trib"], "trib")
                    wload(trilo[:], I["trilo"], "trilo")
                    wload(tcs[:], I["tc"], "tcs")
                    P.dma("act", gpost[:], I["gpost_rep"], W=["gpost"])
                    P.dma("act", gffn[:], I["gffn_b"], W=["gffn"])
                    h1 = sb("h1", [64, 256], BF16)
                    MEMSET("dve", ["h1"], h1[:], 0.0)
                    csb = sb("csb", [64, 1], F32)
                    NS = 3
                    hTt = [sb(f"p2hT{i}", [128, 8, 128], BF16) for i in range(NS)]

                    def loads2(n):
                        s = n % NS
                        P.dma("sp", hTt[s][:], hT_scr[n].rearrange("p (k t) -> p k t", t=128), R=[("hT", n)], W=[f"p2hT{s}"])
                    loads2(0)
                    loads2(1)
                    for n in range(NT_CTX):
                        if n + 2 < NT_CTX:
                            loads2(n + 2)
                        s = n % NS
                        hk = f"p2hT{s}"
                        ht = hTt[s]
                        tok = slice(n * 128, (n + 1) * 128)
                        b0 = 3 * (n % 2)
                        for j, dst in enumerate((kcT, vcT)):
                            for kc in range(8):
                                MM([hk, "wkv"], [FK[b0]], Fb[b0][:, j * 128:(j + 1) * 128], wkv[:, kc, j * 128:(j + 1) * 128], ht[:, kc, :], kc == 0, kc == 7)
                        CP("act", [FK[b0]], ["kcT"], kcT[:, tok], Fb[b0][:, 0:128])
                        CP("act", [FK[b0]], ["vcT"], vcT[:, tok], Fb[b0][:, 128:256])
                        for v in range(4):
                            c0 = (256, 320, 512, 576)[v]
                            for kc in range(8):
                                MM([hk, "wkv"], [FK[b0 + 1]], Fb[b0 + 1][0:64, v * 128:(v + 1) * 128], wkv[:, kc, c0:c0 + 64], ht[:, kc, :], kc == 0, kc == 7)
                        CP("dve", [FK[b0 + 1]], ["KT"], KT[0:64, :, tok], Fb[b0 + 1][0:64, :].rearrange("p (v t) -> p v t", t=128))
                        for j, c0 in enumerate((384, 640)):
                            for kc in range(8):
                                MM([hk, "wkv"], [FK[b0 + 2]], Fb[b0 + 2][:, j * 128:(j + 1) * 128], ht[:, kc, :], wkv[:, kc, c0:c0 + 128], kc == 0, kc == 7)
                        CP("dve", [FK[b0 + 2]], ["VA"], VA[:, n, :, 0:64], Fb[b0 + 2][:, 0:256].rearrange("p (v d) -> p v d", d=64))
                    for g in range(2):
                        gs = slice(g * 64, (g + 1) * 64)
                        for kv, src in (("k", kcT), ("v", vcT)):
                            bk = 6 if kv == "k" else 7
                            view = src[:].rearrange("p (n s) -> p n s", s=16)
                            for l in range(32):
                                MM([kv + "cT", "w1" + kv + "0", "w1" + kv + "1"], [FK[bk]], Fb[bk][0:64, 0:255], w1[kv][gs, l, :],
                                   view[gs, (l // 16):(l // 16) + 255, l % 16], l == 0, False)
                                MM(["pe" + kv, "w1" + kv + "0", "w1" + kv + "1"], [FK[bk]], Fb[bk][0:64, 256:258], w1[kv][gs, l, :],
                                   pe[kv][gs, l, :], False, l == 31)
                            CP("dve", [FK[bk]], ["csb"], csb[:], Fb[bk][0:64, 256:257])
                            ACT([FK[bk], "csb"], ["h1"], out=h1[:, 0:255], in_=Fb[bk][0:64, 0:255], func=AF.Silu, bias=csb[:, 0:1])
                            if kv == "k":
                                MM(["h1", "w2k"], [FK[5]], Fb[5][0:64, 0:256], w2["k"][:, :], h1[:, :], True, True)
                                CP("dve", [FK[5]], ["Kc"], Kc[0:64, g, :], Fb[5][0:64, 0:256])
                            else:
                                for nt in range(2):
                                    MM(["h1", "w2v"], [FK[5]], Fb[5][:, nt * 64:(nt + 1) * 64], h1[:, nt * 128:(nt + 1) * 128], w2["v"][:, :], True, True)
                                CP("dve", [FK[5]], ["Vc"], Vc[:, g, :, 0:64], Fb[5][:, 0:128].rearrange("p (t d) -> p t d", d=64))
                P.fence()

                if stop_after >= 3:
                    with contextlib.ExitStack() as st:
                        sb = lambda name, shape, dt: st.enter_context(nc.sbuf_tensor("sb_" + name, list(shape), dt))
                        NS = 3
                        shc = [sb(f"shc{i}", [16, 256], BF16) for i in range(NS)]
                        selA = [sb(f"selA{i}", [128, 64], F32) for i in range(NS)]
                        selB = [sb(f"selB{i}", [128, 64], F32) for i in range(NS)]
                        Qw = [sb(f"Qw{i}", [128, 2, 512], BF16) for i in range(NS)]
                        Qs = [sb(f"Qs{i}", [128, 2, 512], BF16) for i in range(NS)]
                        QsB = [sb(f"QsB{i}", [128, 2, 512], BF16) for i in range(2)]
                        for i in range(NS):
                            MEMSET("dve", [f"Qs{i}sel0", f"Qs{i}sel1", f"Qs{i}q", f"Qs{i}aug"], Qs[i][:], 0.0)
                        for i in range(2):
                            MEMSET("pool", [f"QsB{i}sel0", f"QsB{i}sel1", f"QsB{i}q", f"QsB{i}aug"], QsB[i][:], 0.0)
                        selE = [[sb(f"selE{g}{v}", [128, 128], BF16) for v in range(2)] for g in range(2)]
                        for g in range(2):
                            for v in range(2):
                                MEMSET("pool", [f"selE{g}{v}"], selE[g][v][:], 0.0)
                        hTt = [sb(f"p3hT{i}", [128, 8, 128], BF16) for i in range(NS)]
                        NPE = 4
                        Pe = [sb(f"Pe{i}", [128, 512], BF16) for i in range(NPE)]
                        tg_ = sb("tg_", [128, 24], F32)
                        gsig = [sb(f"gsig{i}", [128, 24], F32) for i in range(2)]
                        tmg = sb("tmg", [128, 1024], F32)
                        pgin = [sb(f"pgin{i}", [128, 1024], F32) for i in range(2)]
                        xt = [sb(f"p3xt{i}", [128, 1024], F32) for i in range(2)]
                        x1 = [sb(f"x1_{i}", [128, 1024], F32) for i in range(2)]
                        h2T = [sb(f"h2T{i}", [128, 1024], BF16) for i in range(2)]
                        hb2 = sb("hb2", [128, 1024], BF16)
                        tmp1 = sb("tmp1", [128, 1024], F32)
                        mixed = sb("mixed", [128, 1024], BF16)
                        mixT = sb("mixT", [128, 8, 128], BF16)
                        onsa = [sb(f"onsa{i}", [128, 512], BF16) for i in range(2)]
                        onT = sb("onT", [128, 4, 128], BF16)
                        denc = sb("denc", [128, 4], F32)
                        rdc = sb("rdc", [128, 4], F32)
                        rdw = sb("rdw", [128, 4], F32)
                        rds = sb("rds", [128, 4], F32)
                        cfc = sb("cfc", [128, 4], F32)
                        cfw = sb("cfw", [128, 4], F32)
                        cfs = sb("cfs", [128, 4], F32)
                        ocs = [sb(f"ocs{g}", [128, 4, 64], F32) for g in range(2)]
                        ows = [sb(f"ows{g}", [128, 4, 64], F32) for g in range(2)]
                        tmpi = sb("tmpi", [128, 4, 64], F32)
                        tmpo = sb("tmpo", [128, 4, 64], F32)
                        imp = sb("imp", [128, 64], F32)
                        sc1 = sb("sc1", [128, 64], F32)
                        sc2 = sb("sc2", [128, 64], F32)
                        m8 = sb("m8", [128, 16], F32)
                        ssy = sb("ssy", [128, 2], F32)
                        ssy1 = sb("ssy1", [128, 1], F32)
                        ty = sb("ty", [128, 1], F32)
                        rstdy = sb("rstdy", [128, 1], F32)
                        ssh = sb("ssh", [128, 1], F32)
                        st6h = sb("st6h", [128, 12], F32)
                        mvh = sb("mvh", [128, 2], F32)
                        th = sb("th", [128, 1], F32)
                        rstdh = sb("rstdh", [128, 1], F32)
                        junk3 = sb("junk3", [128, 1024], BF16)
                        O0 = NT_CTX - NT_OWN
                        pe_rr = [0]
                        sc_rr = [0]
                        pending = []

                        class Job:
                            def __init__(self, score, pv, pre=None, after=None):
                                self.score, self.pv, self.pre, self.after = score, pv, pre, after
                                self.k = None

                        def emit_pv(jb):
                            for (out_ap, bkey, r, rhs_ap, rkey, st_, sp_) in jb.pv:
                                MM([f"Pe{jb.k}"] + (VCK if rkey == "Vcall" else [rkey]), [bkey], out_ap, Pe[jb.k][:, r * 128:(r + 1) * 128], rhs_ap, st_, sp_)
                            if jb.after is not None:
                                jb.after()

                        SCB = (0, 1, 3)

                        def run_jobs(jobs):
                            q_ = []
                            L0, J0 = len(pending), len(jobs)
                            for ji, jb in enumerate(jobs):
                                if jb.pre is not None:
                                    jb.pre()
                                b = SCB[sc_rr[0] % 3]
                                sc_rr[0] += 1
                                ns = len(jb.score)
                                for idx, (lhsT, rhs, R) in enumerate(jb.score):
                                    MM(R, [FK[b]], Fb[b][:, :], lhsT, rhs, idx == 0, idx == ns - 1)
                                if len(q_) >= 2:
                                    emit_pv(q_.pop(0))
                                k = pe_rr[0] % NPE
                                pe_rr[0] += 1
                                ACT([FK[b]], [f"Pe{k}"], out=Pe[k][:], in_=Fb[b][:, :], func=AF.Exp)
                                jb.k = k
                                q_.append(jb)
                                for _ in range(((ji + 1) * L0) // J0 - (ji * L0) // J0):
                                    pending.pop(0)()
                            while q_:
                                emit_pv(q_.pop(0))

                        def loads3(i):
                            s = i % NS
                            sB = i % 2
                            m = O0 + i
                            P.dma("sp", hTt[s][:], hT_scr[m].rearrange("p (k t) -> p k t", t=128), R=[("hT", m)], W=[f"p3hT{s}"])
                            P.dma("sp", selA[s][:], I["selA"][i], W=[f"selA{s}"])
                            P.dma("sp", selB[s][:], I["selB"][i], W=[f"selB{s}"])
                            wload(shc[s][:], I["shc"][i], f"shc{s}")
                            for g in range(2):
                                wload(Qw[s][64:69, g, :], I["qaug"][g, i], f"Qw{s}aug")
                                wload(Qs[s][123:128, g, :], I["qaug"][g, i], f"Qs{s}aug")
                                if m >= 29:
                                    wload(QsB[sB][123:128, g, :], I["qaug"][g, i], f"QsB{sB}aug")

                        def tile(i):
                            s = i % NS
                            sB = i % 2
                            d = i % 2
                            m = O0 + i
                            useB = m >= 29
                            hk = f"p3hT{s}"
                            ht = hTt[s]
                            P.dma("sp", xt[d][:], I["xc"][m * 128:(m + 1) * 128, :], W=[f"p3xt{d}"])
                            P.dma("sp", pgin[d][:], mixA_scr[i], R=[("mixA", i)], W=[f"pgin{d}"])
                            for g in range(2):
                                for r in range(4):
                                    h = g * 4 + r
                                    for kc in range(8):
                                        MM([hk, "wnq"], [FK[g]], Fb[g][0:64, r * 128:(r + 1) * 128], wnq[:, kc, h * 64:(h + 1) * 64], ht[:, kc, :], kc == 0, kc == 7)
                                ACT([FK[g]], [f"Qw{s}q"], out=Qw[s][0:64, g, :], in_=Fb[g][0:64, :], func=AF.Identity, scale=0.125)
                                TS("dve", [FK[g]], [f"Qs{s}q"], Qs[s][0:64, g, :], Fb[g][0:64, :], 0.125, None, ALU.mult)
                                if useB:
                                    TS("dve", [FK[g]], [f"QsB{sB}q"], QsB[sB][0:64, g, :], Fb[g][0:64, :], 0.125, None, ALU.mult)
                            for kc in range(8):
                                MM([hk, "wg"], [FK[4]], Fb[4][:, 264:288], ht[:, kc, :], wg[:, kc, :], kc == 0, kc == 7)
                            ACT([FK[4]], ["tg_"], out=tg_[:], in_=Fb[4][:, 264:288], func=AF.Tanh, scale=0.5)
                            TS("dve", ["tg_"], [f"gsig{d}"], gsig[d][:], tg_[:], 0.5, 0.5, ALU.mult, ALU.add)
                            gsv = gsig[d][:].rearrange("p (h b) -> p h b", b=3)
                            accCI = Fb[2][:, :].rearrange("p (r c) -> p r c", c=128)
                            accC = accCI[:, :, 0:64]
                            accI = accCI[:, :, 64:128]
                            accS = Fb[4][:, 0:260].rearrange("p (r c) -> p r c", c=65)
                            accW = Fb[5][:, 0:260].rearrange("p (r c) -> p r c", c=65)
                            bc = lambda t: t[:].unsqueeze(2).broadcast_to([128, 4, 64])

                            def after_cmp(g):
                                def f():
                                    P.op("dve", lambda e: e.tensor_reduce(out=denc[:], in_=accI, axis=mybir.AxisListType.X, op=ALU.add), [FK[2]], ["denc"])
                                    TS("dve", ["denc"], ["denc"], denc[:], denc[:], 1e-30, None, ALU.max)
                                    RECIP(["denc"], ["rdc"], rdc[:], denc[:])
                                    TT("dve", [FK[2], "rdc"], ["tmpi"], tmpi[:], accI, bc(rdc), ALU.mult)
                                    P.op("dve", lambda e: e.tensor_reduce(out=imp[:], in_=tmpi[:].rearrange("p r j -> p j r"), axis=mybir.AxisListType.X, op=ALU.add), ["tmpi"], ["imp"])
                                    TT("dve", ["imp", f"selA{s}"], ["sc1"], sc1[:], imp[:], selA[s][:], ALU.max)
                                    TT("dve", ["sc1", f"selB{s}"], ["sc1"], sc1[:], sc1[:], selB[s][:], ALU.min)
                                    P.op("dve", lambda e: e.max(out=m8[:, 0:8], in_=sc1[:]), ["sc1"], ["m8a"])
                                    P.op("dve", lambda e: e.match_replace(out=sc2[:], in_to_replace=m8[:, 0:8], in_values=sc1[:], imm_value=-3.0e38), ["sc1", "m8a"], ["sc2"])
                                    P.op("dve", lambda e: e.max(out=m8[:, 8:16], in_=sc2[:]), ["sc2"], ["m8b"])
                                    TS("dve", ["sc1", "m8b"], [f"selE{g}0"], selE[g][0][:, 64:123], sc1[:, 0:59], m8[:, 15:16], NEGB, ALU.is_lt, ALU.mult)
                                    if useB:
                                        TS("dve", ["sc1", "m8b"], [f"selE{g}1"], selE[g][1][:, 64:123], sc1[:, 5:64], m8[:, 15:16], NEGB, ALU.is_lt, ALU.mult)
                                    TT("dve", ["rdc", f"gsig{d}"], ["cfc"], cfc[:], rdc[:], gsv[:, g * 4:(g + 1) * 4, 0], ALU.mult)
                                    TT("dve", [FK[2], "cfc"], [f"ocs{g}"], ocs[g][:], accC, bc(cfc), ALU.mult)
                                return f

                            def after_win(g):
                                def f():
                                    RECIP([FK[5]], ["rdw"], rdw[:], accW[:, :, 64])
                                    TT("dve", ["rdw", f"gsig{d}"], ["cfw"], cfw[:], rdw[:], gsv[:, g * 4:(g + 1) * 4, 2], ALU.mult)
                                    TT("dve", [FK[5], "cfw"], [f"ows{g}"], ows[g][:], accW[:, :, 0:64], bc(cfw), ALU.mult)
                                return f

                            def pre_sel(g):
                                def f():
                                    TR([f"selE{g}0", "id16"], [FK[2]], Fbf[2][:, 0:128], selE[g][0][:, :], id16[:])
                                    CP("dve", [FK[2]], [f"Qs{s}sel{g}"], Qs[s][64:123, g, :].rearrange("p (r q) -> p r q", q=128),
                                       Fbf[2][64:123, 0:128].unsqueeze(1).broadcast_to([59, 4, 128]))
                                    if useB:
                                        TR([f"selE{g}1", "id16"], [FK[2]], Fbf[2][:, 128:256], selE[g][1][:, :], id16[:])
                                        CP("dve", [FK[2]], [f"QsB{sB}sel{g}"], QsB[sB][64:123, g, :].rearrange("p (r q) -> p r q", q=128),
                                           Fbf[2][64:123, 128:256].unsqueeze(1).broadcast_to([59, 4, 128]))
                                return f

                            def after_sel(g):
                                def f():
                                    RECIP([FK[4]], ["rds"], rds[:], accS[:, :, 64])
                                    TT("dve", ["rds", f"gsig{d}"], ["cfs"], cfs[:], rds[:], gsv[:, g * 4:(g + 1) * 4, 1], ALU.mult)
                                    TT("dve", [FK[4], "cfs"], ["tmpo"], tmpo[:], accS[:, :, 0:64], bc(cfs), ALU.mult)
                                    TT("dve", ["tmpo", f"ocs{g}"], ["tmpo"], tmpo[:], tmpo[:], ocs[g][:], ALU.add)
                                    TT("dve", ["tmpo", f"ows{g}"], [f"onsa{d}"], onsa[d][:, g * 256:(g + 1) * 256].rearrange("p (r c) -> p r c", c=64), tmpo[:], ows[g][:], ALU.add)
                                return f

                            jobs = []
                            qwk = [f"Qw{s}q", f"Qw{s}aug"]
                            for g in range(2):
                                qw = Qw[s][0:69, g, :]
                                for nt in range(2):
                                    score = [(Kc[0:69, g, nt * 128:(nt + 1) * 128], qw, KCK + qwk),
                                             (shc[s][:, nt * 128:(nt + 1) * 128], tcs[:, :], [f"shc{s}", "tcs"])]
                                    pv = []
                                    for r in range(4):
                                        pv.append((Fb[2][:, r * 128:(r + 1) * 128], FK[2], r, Vc[:, g, nt, :], "Vcall", nt == 0 and r == 0, nt == 1 and r == 3))
                                    jobs.append(Job(score, pv, after=after_cmp(g) if nt == 1 else None))
                                for kt in range(m - 4, m + 1):
                                    ks = slice(kt * 128, (kt + 1) * 128)
                                    score = [(KT[0:69, 2 + g, ks], qw, KTK + qwk)]
                                    if kt == m - 4:
                                        score.append((id16[:, :], trilo[:, :], ["id16", "trilo"]))
                                    if kt == m:
                                        score.append((id16[:, :], trib[:, :], ["id16", "trib"]))
                                    pv = [(Fb[5][:, r * 65:(r + 1) * 65], FK[5], r, VA[:, kt, 2 + g, :], "VA", kt == m - 4 and r == 0, kt == m and r == 3) for r in range(4)]
                                    jobs.append(Job(score, pv, after=after_win(g) if kt == m else None))
                            for g in range(2):
                                for kt in range(m + 1):
                                    ks = slice(kt * 128, (kt + 1) * 128)
                                    if kt >= 29:
                                        qa, qk_ = QsB[sB][:, g, :], [f"QsB{sB}q", f"QsB{sB}aug", f"QsB{sB}sel{g}"]
                                    else:
                                        qa, qk_ = Qs[s][:, g, :], [f"Qs{s}q", f"Qs{s}aug", f"Qs{s}sel{g}"]
                                    score = [(KT[:, g, ks], qa, KTK + qk_)]
                                    if kt == m:
                                        score.append((id16[:, :], trib[:, :], ["id16", "trib"]))
                                    pv = [(Fb[4][:, r * 65:(r + 1) * 65], FK[4], r, VA[:, kt, g, :], "VA", kt == 0 and r == 0, kt == m and r == 3) for r in range(4)]
                                    jobs.append(Job(score, pv, pre=pre_sel(g) if kt == 0 else None, after=after_sel(g) if kt == m else None))
                            run_jobs(jobs)

                            def tr_onsa():
                                for kc in range(4):
                                    TR([f"onsa{d}", "id16"], [FK[6]], Fbf[6][:, kc * 128:(kc + 1) * 128], onsa[d][:, kc * 128:(kc + 1) * 128], id16[:])
                                if dbg:
                                    finals.append(P.dma("sp", dbgt["onsa"][i], onsa[d][:], R=[f"onsa{d}"]))

                            def mm_wmb(hf, k0=0, k1=8):
                                hs = slice(hf * 512, (hf + 1) * 512)
                                for kc in range(k0, k1):
                                    MM([hk, "wmb"], [FK[7]], Fb[7][:, :], ht[:, kc, :], wmb[:, kc, hs], kc == 0, kc == 7)

                            def mm_wpn(hf, k0=0, k1=4):
                                hs = slice(hf * 512, (hf + 1) * 512)
                                for kc in range(k0, k1):
                                    MM(["onT", "wpn"], [FK[6]], Fb[6][:, :], onT[:, kc, :], wpn[:, kc, hs], kc == 0, kc == 3)

                            def tanh_mg(hf):
                                hs = slice(hf * 512, (hf + 1) * 512)
                                ACT([FK[7]], ["tmg"], out=tmg[:, hs], in_=Fb[7][:, :], func=AF.Tanh, scale=0.5)

                            def mix(hf):
                                hs = slice(hf * 512, (hf + 1) * 512)
                                STT(["tmg", FK[6]], ["tmp1"], tmp1[:, hs], tmg[:, hs], 1.0, Fb[6][:, :], ALU.add, ALU.mult)
                                TT("dve", ["tmp1", f"pgin{d}"], ["mixed"], mixed[:, hs], tmp1[:, hs], pgin[d][:, hs], ALU.add)

                            def tr_mixed(k0=0, k1=8):
                                for kc in range(k0, k1):
                                    TR(["mixed", "id16"], [FK[7]], Fbf[7][:, kc * 128:(kc + 1) * 128], mixed[:, kc * 128:(kc + 1) * 128], id16[:])

                            def mm_wo(hf, k0, k1):
                                if True:
                                    hs = slice(hf * 512, (hf + 1) * 512)
                                    for kc in range(k0, k1):
                                        MM(["mixT", "wo"], [FK[6 + hf]], Fb[6 + hf][:, :], mixT[:, kc, :], wo[:, kc, hs], kc == 0, kc == 7)

                            def sq_y():
                                for hf in range(2):
                                    ACT([FK[6 + hf]], ["junk3", "yss"], out=junk3[:, 0:512], in_=Fb[6 + hf][:, :], func=AF.Square, accum_out=ssy[:, hf:hf + 1])

                            def x1_step():
                                TT("dve", ["yss"], ["y1ss"], ssy1[:], ssy[:, 0:1], ssy[:, 1:2], ALU.add)
                                rstd_pow(ssy1[:], ty[:], rstdy[:], 1.0 / 1024.0, 4.0 * EPS, "y1", 1)
                                for hf in range(2):
                                    hs = slice(hf * 512, (hf + 1) * 512)
                                    STT([FK[6 + hf], "y1rstd", "gpost"], ["tmp1"], tmp1[:, hs], Fb[6 + hf][:, :], rstdy[:, 0:1], gpost[:, hs], ALU.mult, ALU.mult)
                                TT("dve", ["tmp1", f"p3xt{d}"], [f"x1_{d}"], x1[d][:], tmp1[:], xt[d][:], ALU.add)
                                f = P.dma("sp", x1_scr[i], x1[d][:], R=[f"x1_{d}"], W=[("x1", i)])
                                if dbg:
                                    finals.append(f)

                            def sq_h():
                                ACT([f"x1_{d}"], ["junk3", "hss"], out=junk3[:], in_=x1[d][:], func=AF.Square, accum_out=ssh[:, 0:1])

                            def hb_step():
                                ACT([f"x1_{d}", "hrstd"], ["hb2"], out=hb2[:], in_=x1[d][:], func=AF.Identity, scale=rstdh[:, 0:1])

                            def tr_h(k0=0, k1=8):
                                for kc in range(k0, k1):
                                    TR(["hb2", "id16"], [FK[7]], Fbf[7][:, kc * 128:(kc + 1) * 128], hb2[:, kc * 128:(kc + 1) * 128], id16[:])

                            def h2_step():
                                TT("dve", [FK[7], "gffn"], [f"h2T{d}"], h2T[d][:], Fbf[7][:, 0:1024], gffn[:], ALU.mult)
                                P.dma("sp", h2T_scr[i], h2T[d][:], R=[f"h2T{d}"], W=[("h2T", i)])

                            nop = lambda: None
                            seq = lambda *fs: (lambda: [f() for f in fs])
                            L_ = lambda f, *a: (lambda: f(*a))
                            pending.extend([
                                nop, nop, tr_onsa, nop,
                                seq(lambda: CP("act", [FK[6]], ["onT"], onT[:].rearrange("p k t -> p (k t)"), Fbf[6][:, 0:512]), L_(mm_wmb, 0, 0, 2)),
                                L_(mm_wmb, 0, 2, 4), L_(mm_wmb, 0, 4, 6), L_(mm_wmb, 0, 6, 8),
                                L_(mm_wpn, 0, 0, 2), seq(L_(tanh_mg, 0), L_(mm_wpn, 0, 2, 4)),
                                L_(mm_wmb, 1, 0, 2),
                                seq(L_(mix, 0), L_(mm_wmb, 1, 2, 4)), L_(mm_wmb, 1, 4, 6), L_(mm_wmb, 1, 6, 8),
                                L_(mm_wpn, 1, 0, 2), seq(L_(tanh_mg, 1), L_(mm_wpn, 1, 2, 4)),
                                nop,
                                L_(mix, 1),
                                nop, nop,
                                L_(tr_mixed, 0, 4), L_(tr_mixed, 4, 8),
                                nop,
                                lambda: CP("act", [FK[7]], ["mixT"], mixT[:].rearrange("p k t -> p (k t)"), Fbf[7][:, 0:1024]),
                                nop,
                                L_(mm_wo, 0, 0, 2), L_(mm_wo, 0, 2, 4), L_(mm_wo, 0, 4, 6), L_(mm_wo, 0, 6, 8),
                                L_(mm_wo, 1, 0, 2), L_(mm_wo, 1, 2, 4), L_(mm_wo, 1, 4, 6), L_(mm_wo, 1, 6, 8),
                                nop,
                                sq_y,
                                nop,
                                x1_step,
                                nop, nop, nop,
                                sq_h,
                                nop,
                                lambda: rstd_pow(ssh[:], th[:], rstdh[:], 1.0 / 1024.0, EPS, "h", 1),
                                nop, nop,
                                hb_step,
                                nop,
                                L_(tr_h, 0, 4), L_(tr_h, 4, 8),
                                nop,
                                h2_step,
                            ])

                        loads3(0)
                        loads3(1)
                        for i in range(NT_OWN):
                            tile(i)
                            if i + 2 < NT_OWN:
                                loads3(i + 2)
                        while pending:
                            pending.pop(0)()
                    P.fence()

        if stop_after >= 4:
            with contextlib.ExitStack() as st:
                sb = lambda name, shape, dt: st.enter_context(nc.sbuf_tensor("sb_" + name, list(shape), dt))
                wd = sb("wd", [128, NFC, 1024], BF16)
                gp2 = sb("gp2", [128, 1024], F32)
                P.dma("act", gp2[:], I["gpost2_rep"], W=["gp2"])
                h2 = sb("h2", [128, 8, 1024], BF16)
                actT = sb("actT", [128, NFC, 1024], BF16)
                wgc = [sb(f"wgc{i}", [128, 8, 128], BF16) for i in range(3)]
                wuc = [sb(f"wuc{i}", [128, 8, 128], BF16) for i in range(3)]
                sg = [sb(f"sg{i}", [128, 512], F32) for i in range(2)]
                x1t = [sb(f"x1t{i}", [128, 1024], F32) for i in range(2)]
                ot = [sb(f"ot{i}", [128, 1024], F32) for i in range(2)]
                tmp4 = sb("tmp4", [128, 1024], F32)
                junk4 = sb("junk4", [128, 512], BF16)
                ssf = [sb(f"ssf{i}", [128, 2], F32) for i in range(2)]
                ssf1 = [sb(f"ssf1{i}", [128, 1], F32) for i in range(2)]
                tf = [sb(f"tf{i}", [128, 1], F32) for i in range(2)]
                rstdf = [sb(f"rstdf{i}", [128, 1], F32) for i in range(2)]
                it = 0
                wd_loaded = False
                def h2load(tg):
                    for j in range(8):
                        i = tg * 8 + j
                        P.dma("sp", h2[:, :, j * 128:(j + 1) * 128], h2T_scr[i].rearrange("p (k t) -> p k t", t=128), R=[("h2T", i)], W=[f"h2_{j}"])
                H2K = [f"h2_{j}" for j in range(8)]
                h2load(0)
                for tg in range(2):
                    for c in range(NFC):
                        s = it % 3
                        it += 1
                        cs = slice(c * 128, (c + 1) * 128)
                        wload(wgc[s][:].rearrange("p k f -> p (k f)"), I["wfg"][c], f"wgc{s}")
                        wload(wuc[s][:].rearrange("p k f -> p (k f)"), I["wfu"][c], f"wuc{s}")
                        if not wd_loaded and c >= 2:
                            c0 = (c - 2) * 2
                            if c0 < NFC:
                                wload(wd[:, c0:c0 + 2, :], I["wfd"][c0 * 128:(c0 + 2) * 128, :].rearrange("(c p) d -> p c d", p=128), "wd")
                            if c0 + 2 >= NFC:
                                wd_loaded = True
                        for sub in range(2):
                            ts_ = slice(sub * 512, (sub + 1) * 512)
                            bg, bu = (0, 1) if sub == 0 else (2, 3)
                            for kc in range(8):
                                MM(H2K + [f"wgc{s}"], [FK[bg]], Fb[bg][:, :], wgc[s][:, kc, :], h2[:, kc, ts_], kc == 0, kc == 7)
                            for kc in range(8):
                                MM(H2K + [f"wuc{s}"], [FK[bu]], Fb[bu][:, :], wuc[s][:, kc, :], h2[:, kc, ts_], kc == 0, kc == 7)
                            ACT([FK[bg]], [f"sg{sub}"], out=sg[sub][:], in_=Fb[bg][:, :], func=AF.Silu)
                            TT("dve", [FK[bu], f"sg{sub}"], ["actT"], actT[:, c, ts_], Fb[bu][:, :], sg[sub][:], ALU.mult)
                    if tg == 0:
                        h2load(1)
                    for j in range(8):
                        i = tg * 8 + j
                        s = i % 2
                        P.dma("sp", x1t[s][:], x1_scr[i], R=[("x1", i)], W=[f"x1t{s}"])
                        for hf in range(2):
                            hs = slice(hf * 512, (hf + 1) * 512)
                            bk = 4 + 2 * s + hf
                            for c in range(NFC):
                                MM(["actT", "wd"], [FK[bk]], Fb[bk][:, :], actT[:, c, j * 128:(j + 1) * 128], wd[:, c, hs], c == 0, c == NFC - 1)
                            ACT([FK[bk]], ["junk4", f"f{s}ss"], out=junk4[:], in_=Fb[bk][:, :], func=AF.Square, accum_out=ssf[s][:, hf:hf + 1])
                        TT("dve", [f"f{s}ss"], [f"f1{s}ss"], ssf1[s][:], ssf[s][:, 0:1], ssf[s][:, 1:2], ALU.add)
                        rstd_pow(ssf1[s][:], tf[s][:], rstdf[s][:], 1.0 / 1024.0, EPS, f"f1{s}", 1)
                        for hf in range(2):
                            hs = slice(hf * 512, (hf + 1) * 512)
                            bk = 4 + 2 * s + hf
                            STT([FK[bk], f"f1{s}rstd", "gp2"], ["tmp4"], tmp4[:, hs], Fb[bk][:, :], rstdf[s][:, 0:1], gp2[:, hs], ALU.mult, ALU.mult)
                        TT("dve", ["tmp4", f"x1t{s}"], [f"ot{s}"], ot[s][:], tmp4[:], x1t[s][:], ALU.add)
                        finals.append(P.dma("sp", out[i * 128:(i + 1) * 128, :], ot[s][:], R=[f"ot{s}"]))
        if not finals:
            finals.append(len(P.ops) - 1)
        stats = P.emit(final_wait_ops=finals)
    return nc, stats


def const_tables(half):
    T0 = half * 2048
    t = {}
    u = np.arange(4096)
    absu = u - 2048 + T0
    t["kaug"] = np.stack([u % 128, u // 128, np.ones(4096), np.ones(4096), np.where(absu >= 0, 0.0, NEGB)]).astype(np.float32)
    qa = np.zeros((2, 16, 5, 512), np.float32)
    iq = np.arange(128)
    for g in range(2):
        for i in range(16):
            for r in range(4):
                sl = 2.0 ** (-(g * 4 + r + 1))
                c = slice(r * 128, (r + 1) * 128)
                qa[g, i, 0, c] = sl
                qa[g, i, 1, c] = 128 * sl
                qa[g, i, 2, c] = -sl * iq
                qa[g, i, 3, c] = -128 * sl * (16 + i)
                qa[g, i, 4, c] = 1.0
    t["qaug"] = qa
    n = np.arange(256)
    end = 16 * n + 31
    ca = np.stack([end % 128, end // 128, np.ones(256), np.ones(256), np.where(16 * n - 2048 + T0 >= 0, 0.0, NEGB)]).astype(np.float32)
    ca[:, 255] = [0, 0, 1, 1, NEGB]
    t["caug"] = ca
    shc = np.zeros((16, 16, 256), np.float32)
    for i in range(16):
        rel = n - 8 * (16 + i)
        for uu in range(8):
            shc[i, uu, rel + 1 == uu] = 1.0
        shc[i, 8, rel >= 7] = 1.0
    t["shc"] = shc
    tc = np.zeros((16, 512), np.float32)
    for uu in range(8):
        for r in range(4):
            tc[uu, r * 128:(r + 1) * 128] = np.where(iq >= 15 + 16 * uu, 0.0, NEGB)
    tc[8, :] = NEGB
    t["tc"] = tc
    j = np.arange(64)
    ovm = np.clip(np.minimum(16 * n[:, None] + 32, 64 * j[None, :] + 64) - np.maximum(16 * n[:, None], 64 * j[None, :]), 0, None) / 32.0
    ovm[255, :] = 0.0
    t["ov"] = np.ascontiguousarray(ovm.reshape(2, 128, 64).transpose(1, 0, 2)).astype(np.float32)
    selA = np.zeros((16, 128, 64), np.float32)
    selB = np.zeros((16, 128, 64), np.float32)
    for i in range(16):
        tq = 128 * i + iq + T0
        cur = tq // 64
        jb = j[None, :] - 32 + T0 // 64
        forced = (jb >= 0) & ((jb == 0) | (jb == cur[:, None]) | (jb == cur[:, None] - 1))
        bad = (jb < 0) | (jb > cur[:, None])
        selA[i] = np.where(forced, 1e30, -1e30)
        selB[i] = np.where(bad, -1e30, 1e30)
    t["selA"], t["selB"] = selA, selB
    t["eall"] = (u[None, :] // 64 == j[:, None]).astype(np.float32)
    ekt = np.zeros((59, 4096), np.float32)
    jb_ = u // 64
    row = np.where(u // 128 <= 28, jb_, jb_ - 5)
    ekt[row, u] = 1.0
    t["ekt"] = ekt
    ik = np.arange(128)
    trib = np.where(ik[:, None] <= iq[None, :], 0.0, NEGB)
    trilo = np.where(ik[:, None] > iq[None, :], 0.0, NEGB)
    t["trib"] = np.tile(trib, (1, 4)).astype(np.float32)
    t["trilo"] = np.tile(trilo, (1, 4)).astype(np.float32)
    t["ident"] = np.eye(128, dtype=np.float32)
    t["tril"] = (ik[:, None] <= iq[None, :]).astype(np.float32)
    t["gmask"] = np.tile((ik[:, None] <= iq[None, :]).astype(np.float32), (1, 4))
    return t


def featmajor_rep(g):
    return np.ascontiguousarray(np.repeat(g.reshape(8, 128).T[:, :, None], 128, axis=2).reshape(128, 1024)).astype(np.float32)


def chunk_major(w):
    return np.ascontiguousarray(np.asarray(w).reshape(8, 128, NFC, 128).transpose(2, 1, 0, 3).reshape(NFC, 128, 1024))


def make_in_maps(inp):
    f = lambda a: np.ascontiguousarray(np.asarray(a, dtype=np.float32))
    x = f(inp["x"])
    shared = dict(
        w_in=f(inp["w_in"][0]),
        walpha=f(np.concatenate([inp["gla_w_alpha2"][0], inp["gla_b_alpha"][0][None, :]], axis=0)),
        ngrep=f(np.tile(np.tile(inp["gla_norm_g"][0], 4)[None, :], (128, 1))),
        gpre_b=featmajor_rep(f(inp["norm_mix_pre"][0])),
        gffn_b=featmajor_rep(f(inp["norm_ffn_pre"][0])),
        gpost_rep=f(np.tile(inp["norm_mix_post"][0][None, :], (128, 1))),
        gpost2_rep=f(np.tile(inp["norm_ffn_post"][0][None, :], (128, 1))),
        w1k=f(inp["nsa_cmp_w1_k"][0].reshape(32, 64, 64).transpose(1, 0, 2).reshape(64, 2048)), w2k=f(inp["nsa_cmp_w2_k"][0]),
        w1v=f(inp["nsa_cmp_w1_v"][0].reshape(32, 64, 64).transpose(1, 0, 2).reshape(64, 2048)), w2v=f(inp["nsa_cmp_w2_v"][0]),
        pek=f(np.repeat(np.tile(inp["nsa_cmp_pe_k"][0].T, (2, 1))[:, :, None], 2, axis=2)),
        pev=f(np.repeat(np.tile(inp["nsa_cmp_pe_v"][0].T, (2, 1))[:, :, None], 2, axis=2)),
        wpg=f(inp["w_proj_gla"][0]), wpn=f(inp["w_proj_nsa"][0]), wo=f(inp["w_out"][0]),
        wfg=f(chunk_major(inp["w_ffn_gate"][0])), wfu=f(chunk_major(inp["w_ffn_up"][0])), wfd=f(inp["w_ffn_down"][0]),
    )
    tabs = [const_tables(0), const_tables(1)]
    maps = []
    for c in range(8):
        b, half = c // 2, c % 2
        xc = np.zeros((4096, 1024), np.float32)
        xc[2048:] = x[b, half * 2048:(half + 1) * 2048]
        if half == 1:
            xc[:2048] = x[b, 0:2048]
        m = dict(shared)
        m.update(tabs[half])
        m["xc"] = xc
        for k, s in IN_SPECS.items():
            assert m[k].shape == tuple(s), (k, m[k].shape, s)
        maps.append({k: m[k] for k in IN_SPECS})
    return maps


_CACHE = {}


def kernel(**inputs):
    if "nc" not in _CACHE:
        _CACHE["nc"] = build()[0]
    nc = _CACHE["nc"]
    maps = make_in_maps(inputs)
    res = run_bass_kernel_spmd(nc, maps, core_ids=list(range(8)))
    outp = np.zeros((4, 4096, 1024), np.float32)
    for c in range(8):
        b, half = c // 2, c % 2
        outp[b, half * 2048:(half + 1) * 2048] = np.asarray(res.results[c]["out"], dtype=np.float32)
    return outp
```
